# Optimizing a Trainium2 kernel written in Bass

```python
import jax, jax.numpy as jnp
from jax import lax
import numpy as np

D_MODEL = 1024
BATCH = 8
SEQ = 2048
DEPTH = 4

SB_HEADS = 8
SB_HEAD_DIM = 64
SB_WIDTH = SB_HEADS * SB_HEAD_DIM
MLA_HEADS = 8
MLA_NOPE_DIM = 64
MLA_ROPE_DIM = 32
MLA_V_DIM = 64
MLA_QK_DIM = MLA_NOPE_DIM + MLA_ROPE_DIM
MLA_Q_RANK = 384
MLA_KV_RANK = 256
MLA_WIDTH = MLA_HEADS * MLA_V_DIM
D_FF = 4 * D_MODEL
BLOCK_Q = 128
ROPE_THETA = 10000.0
NORM_EPS = 1e-6
N_MOD = 6
IN_WIDTHS = (SB_WIDTH, SB_WIDTH, SB_WIDTH, MLA_Q_RANK, MLA_KV_RANK, MLA_ROPE_DIM, D_MODEL, D_MODEL)
IN_DIM = sum(IN_WIDTHS)

kernel_name = "hybrid_stickbreaking_mla_sqrelu_adaln"


def _rms_norm(x, g):
    xf = x.astype(jnp.float32)
    y = xf * lax.rsqrt(jnp.mean(xf * xf, axis=-1, keepdims=True) + NORM_EPS)
    return y.astype(x.dtype) * g


def _split_cols(p, widths):
    outs, start = [], 0
    for w in widths:
        outs.append(p[..., start:start + w])
        start += w
    return outs


def _rope_tables(positions):
    inv_freq = 1.0 / (ROPE_THETA ** (jnp.arange(0, MLA_ROPE_DIM, 2, dtype=jnp.float32) / MLA_ROPE_DIM))
    ang = positions.astype(jnp.float32)[..., None] * inv_freq
    return jnp.cos(ang), jnp.sin(ang)


def _apply_rope(t, cos, sin):
    half = t.shape[-1] // 2
    t1, t2 = t[..., :half], t[..., half:]
    cs = cos[:, :, None, :].astype(t.dtype)
    sn = sin[:, :, None, :].astype(t.dtype)
    return jnp.concatenate([t1 * cs - t2 * sn, t2 * cs + t1 * sn], axis=-1)


def _stick_breaking_weights(z, mask):
    log_fail = jnp.where(mask, jax.nn.log_sigmoid(-z), 0.0)
    later = lax.cumsum(log_fail, axis=3, reverse=True) - log_fail
    return jnp.where(mask, jnp.exp(jax.nn.log_sigmoid(z) + later), 0.0)


def _softmax_weights(z, mask):
    return jax.nn.softmax(jnp.where(mask, z, -jnp.inf), axis=-1)


def _causal_block_attention(q, k, v, weight_fn, strict):
    seq = q.shape[1]
    scale = q.shape[-1] ** -0.5
    outs = []
    for t0 in range(0, seq, BLOCK_Q):
        end = t0 + BLOCK_Q
        z = jnp.einsum('bqhd,bkhd->bhqk', q[:, t0:end].astype(jnp.float32),
                       k[:, :end].astype(jnp.float32)) * scale
        t_idx = t0 + jnp.arange(BLOCK_Q)[:, None]
        s_idx = jnp.arange(end)[None, :]
        mask = (s_idx < t_idx) if strict else (s_idx <= t_idx)
        w = weight_fn(z, mask)
        outs.append(jnp.einsum('bhqk,bkhd->bqhd', w.astype(v.dtype), v[:, :end]))
    return jnp.concatenate(outs, axis=1)


def setup_inputs(seed: int = 0) -> dict:
    key = jax.random.key(seed)
    ks = jax.random.split(key, 20)

    def nrm(k, shape, fan_in):
        return jax.random.normal(k, shape, jnp.float32) * (fan_in ** -0.5)

    def gain(k, shape):
        return 1.0 + 0.05 * jax.random.normal(k, shape, jnp.float32)

    x = jax.random.normal(ks[0], (BATCH, SEQ, D_MODEL), jnp.float32)
    c = jax.random.normal(ks[1], (BATCH, D_MODEL), jnp.float32)
    offsets = jax.random.randint(ks[2], (BATCH, 1), 0, 1024, dtype=jnp.int32)
    positions = (offsets + jnp.arange(SEQ, dtype=jnp.int32)[None, :]).astype(jnp.int32)
    return {
        "x": x,
        "c": c,
        "positions": positions,
        "w_ada": nrm(ks[3], (DEPTH, D_MODEL, N_MOD * D_MODEL), D_MODEL),
        "b_ada": 0.02 * jax.random.normal(ks[4], (DEPTH, N_MOD * D_MODEL), jnp.float32),
        "g_mix_norm": gain(ks[5], (DEPTH, D_MODEL)),
        "w_in": nrm(ks[6], (DEPTH, D_MODEL, IN_DIM), D_MODEL),
        "g_q_lat": gain(ks[7], (DEPTH, MLA_Q_RANK)),
        "w_q_up": nrm(ks[8], (DEPTH, MLA_Q_RANK, MLA_HEADS * MLA_QK_DIM), MLA_Q_RANK),
        "g_kv_lat": gain(ks[9], (DEPTH, MLA_KV_RANK)),
        "w_kv_up": nrm(ks[10], (DEPTH, MLA_KV_RANK, MLA_HEADS * (MLA_NOPE_DIM + MLA_V_DIM)), MLA_KV_RANK),
        "w_sb_out": nrm(ks[11], (DEPTH, SB_WIDTH, D_MODEL), SB_WIDTH),
        "w_mla_out": nrm(ks[12], (DEPTH, MLA_WIDTH, D_MODEL), MLA_WIDTH),
        "w_mix_out": nrm(ks[13], (DEPTH, D_MODEL, D_MODEL), D_MODEL),
        "g_mlp_norm": gain(ks[14], (DEPTH, D_MODEL)),
        "w_up": nrm(ks[15], (DEPTH, D_MODEL, D_FF), D_MODEL),
        "w_down": nrm(ks[16], (DEPTH, D_FF, D_MODEL), D_FF),
        "g_final": gain(ks[17], (D_MODEL,)),
    }


def reference(x, c, positions, w_ada, b_ada, g_mix_norm, w_in, g_q_lat, w_q_up,
              g_kv_lat, w_kv_up, w_sb_out, w_mla_out, w_mix_out, g_mlp_norm,
              w_up, w_down, g_final):
    B, S, _ = x.shape
    cos, sin = _rope_tables(positions)
    c_act = jax.nn.silu(c)
    for l in range(DEPTH):
        mod = (c_act @ w_ada[l] + b_ada[l])[:, None, :]
        shift1, scale1, gate1, shift2, scale2, gate2 = jnp.split(mod, N_MOD, axis=-1)

        h = _rms_norm(x, g_mix_norm[l]) * (1.0 + scale1) + shift1
        p = h @ w_in[l]
        q_sb, k_sb, v_sb, q_lat, kv_lat, k_rope, gate_sb, gate_mla = _split_cols(p, IN_WIDTHS)

        o_sb = _causal_block_attention(
            q_sb.reshape(B, S, SB_HEADS, SB_HEAD_DIM),
            k_sb.reshape(B, S, SB_HEADS, SB_HEAD_DIM),
            v_sb.reshape(B, S, SB_HEADS, SB_HEAD_DIM),
            _stick_breaking_weights, strict=True)
        o_sb = o_sb.reshape(B, S, SB_WIDTH) @ w_sb_out[l]

        q = (_rms_norm(q_lat, g_q_lat[l]) @ w_q_up[l]).reshape(B, S, MLA_HEADS, MLA_QK_DIM)
        kv = (_rms_norm(kv_lat, g_kv_lat[l]) @ w_kv_up[l]).reshape(
            B, S, MLA_HEADS, MLA_NOPE_DIM + MLA_V_DIM)
        k_nope, v_mla = kv[..., :MLA_NOPE_DIM], kv[..., MLA_NOPE_DIM:]
        q_full = jnp.concatenate(
            [q[..., :MLA_NOPE_DIM], _apply_rope(q[..., MLA_NOPE_DIM:], cos, sin)], axis=-1)
        k_pe = _apply_rope(k_rope[:, :, None, :], cos, sin)
        k_full = jnp.concatenate(
            [k_nope, jnp.broadcast_to(k_pe, (B, S, MLA_HEADS, MLA_ROPE_DIM))], axis=-1)
        o_mla = _causal_block_attention(q_full, k_full, v_mla, _softmax_weights, strict=False)
        o_mla = o_mla.reshape(B, S, MLA_WIDTH) @ w_mla_out[l]

        merged = jax.nn.sigmoid(gate_sb) * o_sb + jax.nn.sigmoid(gate_mla) * o_mla
        x = x + gate1 * (merged @ w_mix_out[l])

        h = _rms_norm(x, g_mlp_norm[l]) * (1.0 + scale2) + shift2
        x = x + gate2 * (jnp.square(jax.nn.relu(h @ w_up[l])) @ w_down[l])

    return _rms_norm(x, g_final)
```

```python
import bisect
import numpy as np
import concourse.bass as bass
import concourse.mybir as mybir

F32 = mybir.dt.float32
BF16 = mybir.dt.bfloat16
I32 = mybir.dt.int32
AF = mybir.ActivationFunctionType
ALU = mybir.AluOpType
DTB = {F32: 4, BF16: 2, I32: 4}


class R:
    __slots__ = ("ap", "space", "p0", "p1", "b0", "b1")

    def __init__(self, ap, space, p0, p1, b0, b1):
        self.ap, self.space, self.p0, self.p1, self.b0, self.b1 = ap, space, p0, p1, b0, b1


class T:
    def __init__(self, arena_ap, space, off, shape, dtype, name=""):
        self.space, self.off, self.shape, self.dtype, self.name = space, off, list(shape), dtype, name
        self.esz = DTB[dtype]
        n = int(np.prod(shape))
        self.nbytes = n * self.esz
        assert off % 4 == 0 and self.nbytes % 4 == 0, (name, off, self.nbytes)
        base = arena_ap[:, off // 4:(off + self.nbytes) // 4]
        if dtype != F32:
            base = base.bitcast(dtype)
        if len(shape) == 2:
            base = base.rearrange("p (a b) -> p a b", a=shape[0], b=shape[1])
        elif len(shape) == 3:
            base = base.rearrange("p (a b c) -> p a b c", a=shape[0], b=shape[1], c=shape[2])
        elif len(shape) == 4:
            base = base.rearrange("p (a b c d) -> p a b c d", a=shape[0], b=shape[1], c=shape[2], d=shape[3])
        self.full = base
        st = []
        acc = 1
        for s in reversed(self.shape):
            st.append(acc)
            acc *= s
        self.strides = list(reversed(st))

    def __call__(self, *idx, p=(0, 128)):
        idx = list(idx) + [None] * (len(self.shape) - len(idx))
        key = [slice(p[0], p[1])]
        lo = 0
        hi = 0
        for i, (ix, s, st) in enumerate(zip(idx, self.shape, self.strides)):
            if ix is None:
                key.append(slice(None))
                hi += (s - 1) * st
            elif isinstance(ix, tuple):
                assert 0 <= ix[0] < ix[1] <= s, (self.name, idx)
                key.append(slice(ix[0], ix[1]))
                lo += ix[0] * st
                hi += (ix[1] - 1) * st
            else:
                assert 0 <= ix < s, (self.name, idx)
                key.append(ix)
                lo += ix * st
                hi += ix * st
        ap = self.full[tuple(key)]
        return R(ap, self.space, p[0], p[1], self.off + lo * self.esz, self.off + (hi + 1) * self.esz)


class _Track:
    def __init__(self, size):
        self.starts = [0]
        self.segs = [[0, size, -1, {}]]

    def _split(self, pos):
        i = bisect.bisect_right(self.starts, pos) - 1
        seg = self.segs[i]
        if seg[0] == pos or pos >= seg[1]:
            return
        new = [pos, seg[1], seg[2], dict(seg[3])]
        seg[1] = pos
        self.starts.insert(i + 1, pos)
        self.segs.insert(i + 1, new)

    def access(self, b0, b1, write, opidx, rkey, deps):
        self._split(b0)
        self._split(b1)
        i = bisect.bisect_left(self.starts, b0)
        n = len(self.segs)
        first = i
        while i < n and self.segs[i][0] < b1:
            seg = self.segs[i]
            if seg[2] >= 0:
                deps.add(seg[2])
            if write:
                deps.update(seg[3].values())
                seg[2] = opidx
                seg[3] = {}
            else:
                seg[3][rkey] = opidx
            i += 1
        if write and i - first > 1:
            keep = self.segs[first]
            keep[1] = self.segs[i - 1][1]
            del self.segs[first + 1:i]
            del self.starts[first + 1:i]


class Op:
    __slots__ = ("idx", "eng", "fn", "deps", "dma", "sig", "cnt", "sem", "out_dram")

    def __init__(self, idx, eng, fn, dma):
        self.idx, self.eng, self.fn, self.dma = idx, eng, fn, dma
        self.deps = set()
        self.sig = False
        self.cnt = 0
        self.sem = None
        self.out_dram = False


class Prog:
    ENGS = ("pe", "act", "dve", "pool", "sp")

    def __init__(self, nc, same_sync=True, n_dma_sems=12):
        self.nc = nc
        self.same_sync = same_sync
        self.n_dma_sems = n_dma_sems
        self.ops = []
        self.tracks = {}
        self.sizes = {}
        self.nrd = 0

    def add_space(self, name, size):
        self.sizes[name] = size
        self.tracks[(name, 0)] = _Track(size)
        self.tracks[(name, 1)] = _Track(size)

    def _touch(self, op, r, write):
        rkey = op.eng if not op.dma else ("dma", op.idx)
        for half in (0, 1):
            lo, hi = half * 64, half * 64 + 64
            if r.p0 < hi and r.p1 > lo:
                self.tracks[(r.space, half)].access(r.b0, r.b1, write, op.idx, rkey, op.deps)

    def op(self, eng, fn, r=(), w=()):
        o = Op(len(self.ops), eng, fn, False)
        for x in r:
            self._touch(o, x, False)
        for x in w:
            self._touch(o, x, True)
        o.deps.discard(o.idx)
        self.ops.append(o)
        return o

    def dma(self, queue, out, in_, r=(), w=(), out_dram=False):
        o = Op(len(self.ops), queue, lambda e, out=out, in_=in_: e.dma_start(out=out, in_=in_), True)
        o.out_dram = out_dram
        for x in r:
            self._touch(o, x, False)
        for x in w:
            self._touch(o, x, True)
        o.deps.discard(o.idx)
        self.ops.append(o)
        return o

    def finalize(self):
        nc = self.nc
        ops = self.ops
        engobj = {"pe": nc.tensor, "act": nc.scalar, "dve": nc.vector, "pool": nc.gpsimd, "sp": nc.sync}
        for o in ops:
            latest = {}
            for d in o.deps:
                od = ops[d]
                if od.dma:
                    continue
                if od.eng != o.eng or o.dma or (self.same_sync and od.eng != "pe"):
                    if latest.get(od.eng, -1) < d:
                        latest[od.eng] = d
            for d in latest.values():
                ops[d].sig = True
            o.deps = set(d for d in o.deps if ops[d].dma) | set(latest.values())
        self._ctx = []

        def mksem(name):
            g = nc.semaphore(name)
            s = g.__enter__()
            self._ctx.append(g)
            return s

        EP = 3968
        esem = {e: [] for e in self.ENGS}
        ecnt = {e: 0 for e in self.ENGS}
        dsem = {}
        dlast = {}
        dn = {}
        waited = {e: {} for e in self.ENGS}
        nwait = 0
        out_ops = []
        for o in ops:
            eng = engobj[o.eng]
            need = {}
            for d in o.deps:
                od = ops[d]
                if od.dma:
                    sem, val = od.sem, od.cnt
                else:
                    if not od.sig:
                        continue
                    if od.eng == o.eng and not o.dma and (od.eng == "pe" or not self.same_sync):
                        continue
                    sem, val = od.sem, od.cnt
                k = id(sem)
                if k not in need or need[k][1] < val:
                    need[k] = (sem, val)
            if o.dma:
                q = o.eng
                i = dn.get(q, 0)
                dn[q] = i + 1
                slot = (q, i % self.n_dma_sems)
                if slot not in dsem:
                    dsem[slot] = [mksem("d_%s_%d_%d" % (slot[0], slot[1], i)), 0]
                mysem, mycnt = dsem[slot]
                if mycnt > 0:
                    k = id(mysem)
                    if k not in need or need[k][1] < mycnt:
                        need[k] = (mysem, mycnt)
                if mycnt + 16 > EP:
                    dsem[slot] = [mksem("d_%s_%d_%d" % (slot[0], slot[1], i)), 0]
                    mysem, mycnt = dsem[slot]
            for k, (wsem, wval) in need.items():
                if waited[o.eng].get(k, 0) >= wval:
                    continue
                waited[o.eng][k] = wval
                eng.wait_ge(wsem, wval)
                nwait += 1
            ins = o.fn(eng)
            if o.dma:
                dsem[slot][1] += 16
                o.sem, o.cnt = mysem, dsem[slot][1]
                ins.then_inc(mysem, 16)
                if o.out_dram:
                    out_ops.append(o)
            elif o.sig:
                c0 = ecnt[o.eng]
                ecnt[o.eng] += 1
                if c0 % EP == 0:
                    esem[o.eng].append(mksem("s_%s_%d" % (o.eng, c0 // EP)))
                o.sem = esem[o.eng][-1]
                o.cnt = c0 % EP + 1
                ins.then_inc(o.sem, 1)
        for o in out_ops:
            k = id(o.sem)
            if waited["sp"].get(k, 0) >= o.cnt:
                continue
            waited["sp"][k] = o.cnt
            nc.sync.wait_ge(o.sem, o.cnt)
        self.stats = dict(n_ops=len(ops), n_wait=nwait, sig={e: ecnt[e] for e in self.ENGS},
                          per_eng={e: sum(1 for o in ops if o.eng == e) for e in self.ENGS})
        return self.stats


from concourse.bass_utils import run_bass_kernel_spmd

D = 1024; S = 2048; NG = 4; GW = 512; NKC = 8; DFF = 4096; L = 4
INW = 4256
SC_SB = 0.125
SC_MLA = 96 ** -0.5
EPS = 1e-6
NEGBIG = -30000.0
ARENA = 212480
STG = 2048


def build(layers, final_norm=True, same_sync=False, dbg=False, stop=None):
    nc = bass.Bass("TRN2", target_bir_lowering=False)

    def din(name, shape, dt=F32):
        return nc.dram_tensor(name, list(shape), dt, kind="ExternalInput").ap()

    xT_d = din("xT", [D, S]); cT_d = din("cT", [128, 8]); pos_d = din("pos", [128, S], I32)
    ropec_d = din("ropec", [128, 2]); consts_d = din("consts", [128, 7 * 128])
    vecs_d = din("vecs", [128, 4 * 69 + 8])
    w_ada = din("w_ada_b", [L, 48, 128, 1024]); w_in = din("w_in_b", [L, 33, 128, 1024]); w_kr = din("w_kr", [L, 128, 512])
    w_q = din("w_q_b", [L, 8, 128, 384]); w_kn = din("w_kn_b", [L, 8, 128, 128]); w_kvv = din("w_kvv_b", [L, 4, 128, 256])
    w_o = din("w_o_b", [L, 8, 128, 1024]); w_mix = din("w_mix_b", [L, 8, 128, 1024])
    w_up = din("w_up_b", [L, 32, 128, 1024]); w_dn = din("w_dn_b", [L, 8, 128, 4096])
    outT_d = nc.dram_tensor("outT", [D, S], F32, kind="ExternalOutput").ap()

    ga = nc.sbuf_tensor("arena", [128, ARENA // 4], F32); arena = ga.__enter__()
    gp = nc.psum_tensor("psum", [128, 4096], F32); psum = gp.__enter__()
    aa = arena[:, :]; pa = psum[:, :]
    P = Prog(nc, same_sync=same_sync, n_dma_sems=8)
    P.add_space("sb", ARENA); P.add_space("ps", 16384)
    PS = [T(pa, "ps", i * 2048, [512], F32, "ps%d" % i) for i in range(8)]
    psi = [0]

    def ps_next():
        b = PS[psi[0] % 6]; psi[0] += 1
        return b
    poi = [0]

    def po_next():
        b = PS[6 + poi[0] % 2]; poi[0] += 1
        return b

    cur = [0]
    dbg_outs = {}

    def tap(name, tile, dt):
        if not dbg:
            return
        n = int(np.prod(tile.shape))
        dt_ = nc.dram_tensor("dbg_" + name, [128, n], dt, kind="ExternalOutput").ap()
        flat = T(aa, "sb", tile.off, [n], dt, "flat_" + name)
        P.dma("sp", dt_[:, :], flat().ap, r=[flat()], out_dram=True)

    def alloc(shape, dt, name=""):
        t = T(aa, "sb", cur[0], shape, dt, name)
        cur[0] += (t.nbytes + 31) // 32 * 32
        assert cur[0] <= ARENA, (name, cur[0])
        return t

    def mm(out, lhsT, rhs, start, stop):
        P.op("pe", lambda e: e.matmul(out.ap, lhsT.ap, rhs.ap, start=start, stop=stop), r=[lhsT, rhs], w=[out])

    def act(out, in_, func, scale=1.0, bias=None, eng="act"):
        rs = [in_]
        kw = {}
        if isinstance(scale, R):
            rs.append(scale); kw["scale"] = scale.ap
        else:
            kw["scale"] = float(scale)
        if isinstance(bias, R):
            rs.append(bias); kw["bias"] = bias.ap
        elif bias is not None:
            kw["bias"] = float(bias)
        P.op("act", lambda e: e.activation(out=out.ap, in_=in_.ap, func=func, **kw), r=rs, w=[out])

    def tt(eng, out, a, b, op):
        P.op(eng, lambda e: e.tensor_tensor(out=out.ap, in0=a.ap, in1=b.ap, op=op), r=[a, b], w=[out])

    def stt(eng, out, in0, scalar, in1, op0, op1):
        rs = [in0, in1]
        sc = scalar
        if isinstance(scalar, R):
            rs.append(scalar); sc = scalar.ap
        P.op(eng, lambda e: e.scalar_tensor_tensor(out=out.ap, in0=in0.ap, scalar=sc, in1=in1.ap, op0=op0, op1=op1), r=rs, w=[out])

    def ts(eng, out, in0, s1, s2, op0, op1=None):
        rs = [in0]
        a1 = s1; a2 = s2
        if isinstance(s1, R):
            rs.append(s1); a1 = s1.ap
        if isinstance(s2, R):
            rs.append(s2); a2 = s2.ap
        if op1 is None:
            P.op(eng, lambda e: e.tensor_scalar(out=out.ap, in0=in0.ap, scalar1=a1, scalar2=None, op0=op0), r=rs, w=[out])
        else:
            P.op(eng, lambda e: e.tensor_scalar(out=out.ap, in0=in0.ap, scalar1=a1, scalar2=a2, op0=op0, op1=op1), r=rs, w=[out])

    def cp(eng, out, in_):
        P.op(eng, lambda e: e.tensor_copy(out=out.ap, in_=in_.ap), r=[in_], w=[out])

    def memset(eng, out, val):
        P.op(eng, lambda e: e.memset(out.ap, val), w=[out])

    xT = alloc([8, S], F32, "xT")
    hT = alloc([8, S], BF16, "hT")
    cos2 = alloc([S], BF16, "cos2"); sinS = alloc([S], BF16, "sinS")
    cb = alloc([7, 128], BF16, "cb")
    onesf = alloc([128], F32, "onesf")
    vecs = alloc([4 * 69 + 8], F32, "vecs")
    cact = alloc([8], BF16, "cact")
    mod = alloc([96], F32, "mod"); gs = alloc([32], F32, "gs")
    ropec = alloc([2], F32, "ropec")
    stg_off = [cur[0], cur[0] + STG * 4]; cur[0] += 2 * STG * 4
    stg_i = [0]
    epst = alloc([1], F32, "eps")
    DYN0 = cur[0]
    IDENT, NU, NONES, NEGS, NEGD, NEGF, ONESB = range(7)

    def wload(dst, src, shape):
        n = int(np.prod(shape)); assert n <= STG, shape
        st = T(aa, "sb", stg_off[stg_i[0] % 2], shape, F32, "stg"); stg_i[0] += 1
        P.dma("sp", st().ap, src, w=[st()])
        cp("dve", dst, st())

    def vec(l, off, n=1):
        return vecs((l * 69 + off, l * 69 + off + n))

    for c in range(8):
        for h2 in range(2):
            P.dma("sp", xT(c, (h2 * 1024, h2 * 1024 + 1024)).ap, xT_d[c * 128:(c + 1) * 128, h2 * 1024:(h2 + 1) * 1024],
                  w=[xT(c, (h2 * 1024, h2 * 1024 + 1024))])
    P.dma("sp", vecs().ap, vecs_d[:, :], w=[vecs()])
    P.dma("sp", ropec().ap, ropec_d[:, :], w=[ropec()])
    wload(cb(), consts_d[:, :].rearrange("p (a b) -> p a b", a=7, b=128), [7, 128])
    memset("dve", onesf(), 1.0)
    cur[0] = DYN0
    tc_ = alloc([8], F32, "tc"); tsg = alloc([8], F32, "tsg")
    P.dma("sp", tc_().ap, cT_d[:, :], w=[tc_()])
    act(tsg(), tc_(), AF.Sigmoid)
    tt("dve", cact(), tc_(), tsg(), ALU.mult)
    RP = (64, 96)
    posi = alloc([S], I32, "posi"); ang = alloc([S], F32, "ang"); u = alloc([S], F32, "u")
    ki = alloc([S], I32, "ki"); kf = alloc([S], F32, "kf"); r1 = alloc([S], F32, "r1"); r2 = alloc([S], F32, "r2")
    P.dma("sp", posi(p=RP).ap, pos_d[64:96, :], w=[posi(p=RP)])
    cp("dve", ang(p=RP), posi(p=RP))
    ts("dve", ang(p=RP), ang(p=RP), ropec((0, 1), p=RP), None, ALU.mult)
    TWO_PI = 2.0 * np.pi
    for which, dst in ((0, sinS), (1, cos2)):
        shift = 0.0 if which == 0 else np.pi / 2
        ts("dve", u(p=RP), ang(p=RP), float(shift), float(1.0 / TWO_PI), ALU.add, ALU.mult)
        cp("dve", ki(p=RP), u(p=RP))
        cp("dve", kf(p=RP), ki(p=RP))
        stt("dve", r1(p=RP), kf(p=RP), float(-TWO_PI), ang(p=RP), ALU.mult, ALU.add)
        if which == 1:
            ts("dve", r1(p=RP), r1(p=RP), float(np.pi / 2), None, ALU.add)
        ts("dve", r2(p=RP), r1(p=RP), float(np.pi), float(-TWO_PI), ALU.is_gt, ALU.mult)
        tt("dve", r1(p=RP), r1(p=RP), r2(p=RP), ALU.add)
        ts("dve", r2(p=RP), r1(p=RP), float(-np.pi), float(TWO_PI), ALU.is_lt, ALU.mult)
        tt("dve", r1(p=RP), r1(p=RP), r2(p=RP), ALU.add)
        ts("dve", r1(p=RP), r1(p=RP), float(3.141592), float(-3.141592), ALU.min, ALU.max)
        if which == 0:
            act(r2(p=RP), r1(p=RP), AF.Sin)
            ts("dve", sinS(p=RP), r2(p=RP), ropec((1, 2), p=RP), None, ALU.mult)
        else:
            act(cos2(p=RP), r1(p=RP), AF.Sin)

    def rstd_from_ss(ps_ss, dst, n):
        act(dst, ps_ss, AF.Ln, scale=1.0 / n, bias=epsb())
        act(dst, dst, AF.Exp, scale=-0.5)

    def epsb():
        return epsc()

    def norm_mod(l, which, sq, rstd, tmp):
        for g in range(NG):
            tg = (g * GW, (g + 1) * GW)
            pss = ps_next()
            for c in range(8):
                act(sq[c % 2](), xT(c, tg), AF.Square)
                mm(pss(), cb(ONESB), sq[c % 2](), c == 0, c == 7)
            rstd_from_ss(pss(), rstd(), D)
            for c in range(8):
                gi = (l % 2) * 16 + which * 8 + c
                mi = (l % 2) * 48 + which * 24 + c
                stt("dve", tmp[c % 2](), xT(c, tg), gs((gi, gi + 1)), rstd(), ALU.mult, ALU.mult)
                act(hT(c, tg), tmp[c % 2](), AF.Identity, bias=mod((mi, mi + 1)))

    memset("dve", epst(), EPS)

    def epsc():
        return epst()
    DYN1 = DYN0

    def mod_work(l):
        wa = [alloc([8, 128], BF16, "wa%d" % i) for i in range(2)]
        psm = po_next()
        mb = (l % 2) * 48
        gb = (l % 2) * 16
        for col in range(48):
            w_ = wa[col % 2]
            wload(w_(), w_ada[l, col].rearrange("p (a b) -> p a b", a=8, b=128), [8, 128])
            for kc in range(8):
                mm(psm((col, col + 1)), w_(kc), cact((kc, kc + 1)), kc == 0, kc == 7)
            yield col
        tt("dve", mod((mb, mb + 48)), psm((0, 48)), vec(l, 21, 48), ALU.add)
        stt("dve", gs((gb, gb + 8)), mod((mb + 8, mb + 16)), 1.0, vec(l, 0, 8), ALU.add, ALU.mult)
        stt("dve", gs((gb + 8, gb + 16)), mod((mb + 32, mb + 40)), 1.0, vec(l, 8, 8), ALU.add, ALU.mult)

    def layer(l):
        mb = (l % 2) * 48
        gb = (l % 2) * 16
        cur[0] = DYN1
        if l == layers[0]:
            for _ in mod_work(l):
                pass
        if stop == 'mod':
            return
        cur[0] = DYN1
        sq = [alloc([GW], BF16, "sq%d" % i) for i in range(2)]
        rstd = alloc([GW], F32, "rstd")
        tmp = [alloc([GW], F32, "tmp%d" % i) for i in range(2)]
        norm_mod(l, 0, sq, rstd, tmp)

        tap("mod", mod, F32); tap("hT", hT, BF16)
        if stop == 'norm1':
            return
        cur[0] = DYN1
        aoM = alloc([4, S], BF16, "aoM")
        MLA0 = cur[0]
        nq = alloc([3, S], BF16, "nq"); nkv = alloc([2, S], BF16, "nkv")
        kpe = alloc([S], BF16, "kpe")
        t1 = alloc([GW], F32, "t1"); t2 = alloc([GW], F32, "t2")
        LAT0 = cur[0]
        wl = alloc([8, 128], BF16, "wl")
        sqb = [alloc([GW], BF16, "sqb%d" % i) for i in range(2)]
        rs2 = alloc([GW], F32, "rs2")
        latps = {}
        for j in range(5):
            wload(wl(), w_in[l, 12 + j].rearrange("p (a b) -> p a b", a=8, b=128), [8, 128])
            for g in range(NG):
                tg = (g * GW, (g + 1) * GW)
                pb = ps_next()
                for kc in range(8):
                    mm(pb(), wl(kc), hT(kc, tg), kc == 0, kc == 7)
                dst = nq(j, tg) if j < 3 else nkv(j - 3, tg)
                act(dst, pb(), AF.Copy)
        for (tile, nch, goff) in ((nq, 3, 16), (nkv, 2, 19)):
            for g in range(NG):
                tg = (g * GW, (g + 1) * GW)
                pss = ps_next()
                for j in range(nch):
                    act(sqb[j % 2](), tile(j, tg), AF.Square)
                    mm(pss(), cb(ONESB), sqb[j % 2](), j == 0, j == nch - 1)
                rstd_from_ss(pss(), rs2(), nch * 128)
                for j in range(nch):
                    stt("dve", tile(j, tg), tile(j, tg), vec(l, goff + j, 1), rs2(), ALU.mult, ALU.mult)
        wkr = alloc([8, 64], BF16, "wkr")
        wload(wkr(), w_kr[l].rearrange("p (a b) -> p a b", a=8, b=64), [8, 64])
        for g in range(NG):
            tg = (g * GW, (g + 1) * GW)
            pA = ps_next(); pB = ps_next()
            for kc in range(8):
                mm(pA(p=RP), wkr(kc, (0, 32)), hT(kc, tg), kc == 0, kc == 7)
            for kc in range(8):
                mm(pB(p=RP), wkr(kc, (32, 64)), hT(kc, tg), kc == 0, kc == 7)
            tt("dve", t1(p=RP), pA(p=RP), cos2(tg, p=RP), ALU.mult)
            tt("dve", t2(p=RP), pB(p=RP), sinS(tg, p=RP), ALU.mult)
            tt("dve", kpe(tg, p=RP), t1(p=RP), t2(p=RP), ALU.add)
        tap("nq", nq, BF16); tap("nkv", nkv, BF16); tap("kpe", kpe, BF16); tap("cos2", cos2, BF16); tap("sinS", sinS, BF16)
        if stop == 'lat':
            return
        cur[0] = LAT0
        wq = [alloc([3, 128], BF16, "wq%d" % i) for i in range(2)]
        wkn = [alloc([2, 64], BF16, "wkn%d" % i) for i in range(2)]
        wkvv = alloc([2, 128], BF16, "wkvv")
        qh = [alloc([S], BF16, "qh%d" % i) for i in range(2)]; kh_ = [alloc([S], BF16, "kh%d" % i) for i in range(2)]
        vp = [alloc([16, 2, 66], BF16, "vp%d" % i) for i in range(2)]
        AT = [alloc([GW], BF16, "AT%d" % i) for i in range(3)]
        rec = alloc([GW], F32, "rec"); rb = alloc([GW], F32, "rb")
        aot = alloc([S], BF16, "aot")

        def mla_proj(h):
            sl = h % 2
            wq_, wkn_, qh_, kk_ = wq[sl], wkn[sl], qh[sl], kh_[sl]
            wload(wq_(), w_q[l, h].rearrange("p (a b) -> p a b", a=3, b=128), [3, 128])
            wload(wkn_(), w_kn[l, h].rearrange("p (a b) -> p a b", a=2, b=64), [2, 64])
            if h % 2 == 0:
                vp_ = vp[(h // 2) % 2]
                wload(wkvv(), w_kvv[l, h // 2].rearrange("p (a b) -> p a b", a=2, b=128), [2, 128])
                memset("pool", vp_(None, None, (64, 65)), 1.0)
                for tt_ in range(16):
                    pv = ps_next()
                    for kc in range(2):
                        mm(pv((0, 128)), nkv(kc, (tt_ * 128, tt_ * 128 + 128)), wkvv(kc), kc == 0, kc == 1)
                    P.op("dve", lambda e, pv=pv, tt_=tt_, vp_=vp_: e.tensor_copy(
                        out=vp_(tt_, None, (0, 64)).ap, in_=pv((0, 128)).ap.rearrange("p (a b) -> p a b", a=2, b=64)),
                        r=[pv((0, 128))], w=[vp_(tt_)])
            for g in range(NG):
                tg = (g * GW, (g + 1) * GW)
                pq = ps_next(); pq2 = ps_next(); pk = ps_next()
                for kc in range(3):
                    mm(pq(p=(0, 96)), wq_(kc, (0, 96)), nq(kc, tg), kc == 0, kc == 2)
                for kc in range(3):
                    mm(pq2(p=RP), wq_(kc, (96, 128)), nq(kc, tg), kc == 0, kc == 2)
                for kc in range(2):
                    mm(pk(p=(0, 64)), wkn_(kc), nkv(kc, tg), kc == 0, kc == 1)
                cp("dve", qh_(tg, p=(0, 64)), pq(p=(0, 64)))
                tt("dve", t1(p=RP), pq(p=RP), cos2(tg, p=RP), ALU.mult)
                tt("dve", t2(p=RP), pq2(p=RP), sinS(tg, p=RP), ALU.mult)
                tt("dve", qh_(tg, p=RP), t1(p=RP), t2(p=RP), ALU.add)
                cp("dve", kk_(tg, p=(0, 64)), pk(p=(0, 64)))
                cp("dve", kk_(tg, p=RP), kpe(tg, p=RP))

        def mla_attn(h):
            sl = h % 2
            hp, hb = h // 2, (h % 2) * 64
            qh_, kk_, vp_ = qh[sl], kh_[sl], vp[hp % 2]
            items = []
            for g in range(NG):
                nkb = 4 * g + 4
                for i, kb in enumerate(range(nkb - 1, -1, -1)):
                    items.append(dict(g=g, i=i, kb=kb, nkb=nkb))
            N = len(items)
            pos_ = {}

            def s1(n):
                it = items[n]
                g, kb = it["g"], it["kb"]
                tg = (g * GW, (g + 1) * GW)
                pz = ps_next()
                dg = kb - 4 * g
                mm(pz(), kk_((kb * 128, kb * 128 + 128), p=(0, 96)), qh_(tg, p=(0, 96)), True, dg < 0)
                if dg >= 0:
                    for qb in range(dg + 1):
                        mm(pz((qb * 128, qb * 128 + 128)), cb(IDENT), cb(NEGD if qb == dg else NEGF), False, qb == dg)
                act(AT[n % 3](), pz(), AF.Exp, scale=SC_MLA)

            def s2(n):
                it = items[n]
                g, kb, i, nkb = it["g"], it["kb"], it["i"], it["nkb"]
                tg = (g * GW, (g + 1) * GW)
                if i == 0:
                    pos_[g] = po_next()
                po = pos_[g]
                mm(po(p=(0, 65)), vp_(kb, h % 2, (0, 65)), AT[n % 3](), i == 0, i == nkb - 1)
                if i == nkb - 1:
                    act(rec(p=(64, 65)), po(p=(64, 65)), AF.Ln)
                    act(rec(p=(64, 65)), rec(p=(64, 65)), AF.Exp, scale=-1.0)
                    pend.append((n, g, po))

            def s3(g, po):
                tg = (g * GW, (g + 1) * GW)
                pr = ps_next()
                mm(pr(p=(0, 64)), onesf((0, 64), p=(64, 65)), rec(p=(64, 65)), True, True)
                cp("dve", rb(p=(0, 64)), pr(p=(0, 64)))
                if hb == 0:
                    tt("dve", aoM(hp, tg, p=(0, 64)), po(p=(0, 64)), rb(p=(0, 64)), ALU.mult)
                else:
                    tt("dve", aot(tg, p=(0, 64)), po(p=(0, 64)), rb(p=(0, 64)), ALU.mult)
            pend = []
            for step in range(N + 1):
                if step < N:
                    s1(step)
                if step >= 1:
                    s2(step - 1)
                if pend and step - 1 - pend[0][0] >= 3:
                    _, g_, po_ = pend.pop(0)
                    s3(g_, po_)
            while pend:
                _, g_, po_ = pend.pop(0)
                s3(g_, po_)
            if hb == 64:
                P.dma("pool", aoM(hp, p=(64, 128)).ap, aot(p=(0, 64)).ap, r=[aot(p=(0, 64))], w=[aoM(hp, p=(64, 128))])

        mla_proj(0)
        for h in range(8):
            if h + 1 < 8:
                mla_proj(h + 1)
            mla_attn(h)

        tap("aoM", aoM, BF16); tap("qh", qh[1], BF16); tap("kh", kh_[1], BF16)
        if stop == 'mla':
            return
        cur[0] = MLA0
        aoS = alloc([4, S], BF16, "aoS")
        MLA0 = cur[0]
        wqkv = [alloc([8, 3, 128], BF16, "wqkv%d" % i) for i in range(2)]
        qp = [alloc([S], BF16, "qp%d" % i) for i in range(2)]; kp = [alloc([S], BF16, "kp%d" % i) for i in range(2)]
        vs = [alloc([16, 128], BF16, "vs%d" % i) for i in range(2)]
        ef = [alloc([GW], F32, "ef%d" % i) for i in range(3)]
        sp_ = [alloc([GW], BF16, "sp%d" % i) for i in range(4)]
        acc = [alloc([GW], BF16, "acc%d" % i) for i in range(2)]
        AT = [alloc([GW], BF16, "ATs%d" % i) for i in range(3)]

        def sb_proj(hp):
            sl = hp % 2
            w_, qp_, kp_, vs_ = wqkv[sl], qp[sl], kp[sl], vs[sl]
            for j in range(3):
                wload(w_(None, j), w_in[l, j * 4 + hp].rearrange("p (a b) -> p a b", a=8, b=128), [8, 128])
            for g in range(NG):
                tg = (g * GW, (g + 1) * GW)
                pq = ps_next(); pk = ps_next()
                for kc in range(8):
                    mm(pq(), w_(kc, 0), hT(kc, tg), kc == 0, kc == 7)
                for kc in range(8):
                    mm(pk(), w_(kc, 1), hT(kc, tg), kc == 0, kc == 7)
                cp("dve", qp_(tg), pq())
                cp("dve", kp_(tg), pk())
            for tt_ in range(16):
                pv = ps_next()
                for kc in range(8):
                    mm(pv((0, 128)), hT(kc, (tt_ * 128, tt_ * 128 + 128)), w_(kc, 2), kc == 0, kc == 7)
                cp("dve", vs_(tt_), pv((0, 128)))

        def sb_attn(hp):
            sl = hp % 2
            qp_, kp_, vs_ = qp[sl], kp[sl], vs[sl]
            items = []
            for hh in range(2):
                for g in range(NG):
                    nkb = 4 * g + 4
                    for i, kb in enumerate(range(nkb - 1, -1, -1)):
                        items.append(dict(hh=hh, g=g, i=i, kb=kb, nkb=nkb))
            N = len(items)
            pos_ = {}

            def zmm(pz, it, last_stop):
                hh, g, kb = it["hh"], it["g"], it["kb"]
                hr = (hh * 64, hh * 64 + 64)
                tg = (g * GW, (g + 1) * GW)
                dg = kb - 4 * g
                mm(pz(), kp_((kb * 128, kb * 128 + 128), p=hr), qp_(tg, p=hr), True, last_stop and dg < 0)
                if dg >= 0:
                    for qb in range(dg + 1):
                        mm(pz((qb * 128, qb * 128 + 128)), cb(IDENT), cb(NEGS if qb == dg else NEGF), False, last_stop and qb == dg)

            def s1(n):
                it = items[n]
                pz = ps_next()
                zmm(pz, it, True)
                e_ = ef[n % 3]
                act(e_(), pz(), AF.Exp, scale=SC_SB)
                act(sp_[n % 4](), e_(), AF.Ln, bias=1.0)

            def s2(n):
                it = items[n]
                i = it["i"]
                s_ = sp_[n % 4]
                pz2 = ps_next()
                mm(pz2(), cb(NU), s_(), True, i == 0)
                if i == 1:
                    mm(pz2(), cb(NONES), sp_[(n - 1) % 4](), False, True)
                    tt("dve", acc[i % 2](), sp_[(n - 1) % 4](), s_(), ALU.add)
                elif i > 1:
                    mm(pz2(), cb(NONES), acc[(i - 1) % 2](), False, True)
                    tt("dve", acc[i % 2](), acc[(i - 1) % 2](), s_(), ALU.add)
                a_ = AT[n % 3]
                act(a_(), pz2(), AF.Exp, scale=SC_SB)
                tt("dve", a_(), a_(), ef[n % 3](), ALU.mult)

            def s3(n):
                it = items[n]
                hh, g, i, kb, nkb = it["hh"], it["g"], it["i"], it["kb"], it["nkb"]
                hr = (hh * 64, hh * 64 + 64)
                tg = (g * GW, (g + 1) * GW)
                if i == 0:
                    pos_[(hh, g)] = po_next()
                po = pos_[(hh, g)]
                mm(po(p=hr), vs_(kb, (hh * 64, hh * 64 + 64)), AT[n % 3](), i == 0, i == nkb - 1)
                if i == nkb - 1:
                    cp("dve", aoS(hp, tg, p=hr), po(p=hr))
            for step in range(N + 2):
                if step < N:
                    s1(step)
                if 1 <= step <= N:
                    s2(step - 1)
                if step >= 2:
                    s3(step - 2)

        sb_proj(0)
        for hp in range(4):
            if hp + 1 < 4:
                sb_proj(hp + 1)
            sb_attn(hp)

        tap("aoS", aoS, BF16)
        if stop == 'sb':
            return
        cur[0] = MLA0
        mg = alloc([8, S], BF16, "mg")
        MG_END = cur[0]
        wgs = [alloc([8, 2, 128], BF16, "wg%d" % i) for i in range(2)]; wos = [alloc([4, 2, 128], BF16, "wo0")] * 2
        s1 = alloc([GW], F32, "s1"); s2 = alloc([GW], F32, "s2"); m1 = s1; m2 = s2
        nxt = mod_work(l + 1) if (l + 1 < L and l + 1 in layers) else None
        def mg_load_g(c_):
            for j in range(2):
                wload(wgs[c_ % 2](None, j), w_in[l, 17 + j * 8 + c_].rearrange("p (a b) -> p a b", a=8, b=128), [8, 128])

        def mg_load_o(c_):
            wload(wos[0](), w_o[l, c_].rearrange("p (a b c) -> p a b c", a=4, b=2, c=128), [4, 2, 128])
        mg_load_g(0); mg_load_o(0)
        for c in range(8):
            wg = wgs[c % 2]; wo = wos[c % 2]
            if c + 1 < 8:
                mg_load_g(c + 1)
            for g in range(NG):
                tg = (g * GW, (g + 1) * GW)
                pg1 = ps_next(); pg2 = ps_next(); po1 = ps_next(); po2 = ps_next()
                for kc in range(8):
                    mm(pg1(), wg(kc, 0), hT(kc, tg), kc == 0, kc == 7)
                for kc in range(8):
                    mm(pg2(), wg(kc, 1), hT(kc, tg), kc == 0, kc == 7)
                for kc in range(4):
                    mm(po1(), wo(kc, 0), aoS(kc, tg), kc == 0, kc == 3)
                for kc in range(4):
                    mm(po2(), wo(kc, 1), aoM(kc, tg), kc == 0, kc == 3)
                if g == NG - 1 and c + 1 < 8:
                    mg_load_o(c + 1)
                act(s1(), pg1(), AF.Sigmoid)
                act(s2(), pg2(), AF.Sigmoid)
                tt("dve", m1(), s1(), po1(), ALU.mult)
                tt("dve", m2(), s2(), po2(), ALU.mult)
                tt("dve", mg(c, tg), m1(), m2(), ALU.add)
                if nxt is not None:
                    for _ in range(2):
                        next(nxt, None)

        tap("mg", mg, BF16)
        if stop == 'merge':
            return
        cur[0] = MG_END
        wm = [alloc([8, 128], BF16, "wm%d" % i) for i in range(2)]
        wload(wm[0](), w_mix[l, 0].rearrange("p (a b) -> p a b", a=8, b=128), [8, 128])
        for c in range(8):
            w_ = wm[c % 2]
            if c + 1 < 8:
                wload(wm[(c + 1) % 2](), w_mix[l, c + 1].rearrange("p (a b) -> p a b", a=8, b=128), [8, 128])
            for g in range(NG):
                tg = (g * GW, (g + 1) * GW)
                pb = ps_next()
                for kc in range(8):
                    mm(pb(), w_(kc), mg(kc, tg), kc == 0, kc == 7)
                stt("dve", xT(c, tg), pb(), mod((mb + 16 + c, mb + 17 + c)), xT(c, tg), ALU.mult, ALU.add)

        if stop == 'mix':
            return
        cur[0] = DYN1
        sq = [alloc([GW], BF16, "sq%d" % i) for i in range(2)]
        rstd = alloc([GW], F32, "rstd")
        tmp = [alloc([GW], F32, "tmp%d" % i) for i in range(2)]
        norm_mod(l, 1, sq, rstd, tmp)

        if stop == 'norm2':
            return
        cur[0] = DYN1
        upT = alloc([32, 1024], BF16, "upT")
        wu = [alloc([8, 128], BF16, "wu%d" % i) for i in range(2)]
        wd = alloc([32, 128], BF16, "wd")
        rl = [alloc([GW], F32, "rl%d" % i) for i in range(2)]
        for th in range(2):
            wload(wu[0](), w_up[l, 0].rearrange("p (a b) -> p a b", a=8, b=128), [8, 128])
            for fc in range(32):
                w_ = wu[fc % 2]
                if fc + 1 < 32:
                    wload(wu[(fc + 1) % 2](), w_up[l, fc + 1].rearrange("p (a b) -> p a b", a=8, b=128), [8, 128])
                for g2 in range(2):
                    tg = (th * 1024 + g2 * GW, th * 1024 + (g2 + 1) * GW)
                    pb = ps_next()
                    for kc in range(8):
                        mm(pb(), w_(kc), hT(kc, tg), kc == 0, kc == 7)
                    r_ = rl[(fc * 2 + g2) % 2]
                    act(r_(), pb(), AF.Relu)
                    tt("dve", upT(fc, (g2 * GW, (g2 + 1) * GW)), r_(), r_(), ALU.mult)
            def wd_load(c_, q2):
                wdb = w_dn[l, c_].rearrange("p (a b) -> p a b", a=32, b=128)
                wload(wd((q2 * 16, q2 * 16 + 16)), wdb[:, q2 * 16:(q2 + 1) * 16, :], [16, 128])
            wd_load(0, 0); wd_load(0, 1)
            for c in range(8):
                pbs = [ps_next(), ps_next()]
                for q2 in range(2):
                    for g2 in range(2):
                        for kc in range(q2 * 16, q2 * 16 + 16):
                            mm(pbs[g2](), wd(kc), upT(kc, (g2 * GW, (g2 + 1) * GW)), kc == 0, kc == 31)
                    if c + 1 < 8:
                        wd_load(c + 1, q2)
                for g2 in range(2):
                    tg = (th * 1024 + g2 * GW, th * 1024 + (g2 + 1) * GW)
                    stt("dve", xT(c, tg), pbs[g2](), mod((mb + 40 + c, mb + 41 + c)), xT(c, tg), ALU.mult, ALU.add)

    for l in layers:
        if stop != 'pro':
            layer(l)

    cur[0] = DYN1
    sq = [alloc([GW], BF16, "sq%d" % i) for i in range(2)]
    rstd = alloc([GW], F32, "rstd")
    ot = [alloc([GW], F32, "ot%d" % i) for i in range(2)]
    for g in range(NG):
        tg = (g * GW, (g + 1) * GW)
        if final_norm:
            pss = ps_next()
            for c in range(8):
                act(sq[c % 2](), xT(c, tg), AF.Square)
                mm(pss(), cb(ONESB), sq[c % 2](), c == 0, c == 7)
            rstd_from_ss(pss(), rstd(), D)
        for c in range(8):
            o_ = ot[c % 2]
            if final_norm:
                stt("dve", o_(), xT(c, tg), vecs((4 * 69 + c, 4 * 69 + c + 1)), rstd(), ALU.mult, ALU.mult)
                P.dma("sp", outT_d[c * 128:(c + 1) * 128, g * GW:(g + 1) * GW], o_().ap, r=[o_()], out_dram=True)
            else:
                P.dma("sp", outT_d[c * 128:(c + 1) * 128, g * GW:(g + 1) * GW], xT(c, tg).ap, r=[xT(c, tg)], out_dram=True)
    stats = P.finalize()
    return nc, stats


_CACHE = {}


def host_inputs(inp, b):
    f = np.float32
    m = {}
    m["xT"] = np.ascontiguousarray(inp["x"][b].T)
    m["cT"] = np.ascontiguousarray(inp["c"][b].reshape(8, 128).T)
    m["pos"] = np.ascontiguousarray(np.broadcast_to(inp["positions"][b][None, :], (128, S))).astype(np.int32)
    return m


def shared_inputs(inp):
    f = np.float32
    m = {}
    ropec = np.zeros((128, 2), f)
    inv = (1.0 / (np.float32(10000.0) ** (np.arange(0, 32, 2, dtype=f) / np.float32(32)))).astype(f)
    for p in range(64, 96):
        ropec[p, 0] = inv[(p - 64) % 16]
        ropec[p, 1] = -1.0 if p < 80 else 1.0
    m["ropec"] = ropec
    j = np.arange(128)[:, None]; t = np.arange(128)[None, :]
    consts = np.zeros((128, 7, 128), f)
    consts[:, 0] = (j == t)
    consts[:, 1] = np.where(j >= t, -1.0 / SC_SB, 0.0)
    consts[:, 2] = -1.0 / SC_SB
    consts[:, 3] = np.where(j >= t, NEGBIG, 0.0)
    consts[:, 4] = np.where(j > t, NEGBIG, 0.0)
    consts[:, 5] = NEGBIG
    consts[:, 6] = 1.0
    m["consts"] = consts.reshape(128, 7 * 128)
    vecs = np.zeros((128, 4 * 69 + 8), f)
    for l in range(L):
        o = l * 69
        vecs[:, o:o + 8] = inp["g_mix_norm"][l].reshape(8, 128).T
        vecs[:, o + 8:o + 16] = inp["g_mlp_norm"][l].reshape(8, 128).T
        vecs[:, o + 16:o + 19] = inp["g_q_lat"][l].reshape(3, 128).T
        vecs[:, o + 19:o + 21] = inp["g_kv_lat"][l].reshape(2, 128).T
        vecs[:, o + 21:o + 69] = inp["b_ada"][l].reshape(48, 128).T
    vecs[:, 4 * 69:] = inp["g_final"].reshape(8, 128).T
    m["vecs"] = vecs
    def blocked(w, nb_cols=None):
        Lh, K, N = w.shape
        return np.ascontiguousarray(w.reshape(Lh, K // 128, 128, N // 128, 128).transpose(0, 3, 2, 1, 4).reshape(Lh, N // 128, 128, (K // 128) * 128), dtype=f)

    w_in_ = inp["w_in"]
    m["w_ada_b"] = blocked(inp["w_ada"])
    m["w_in_b"] = blocked(np.concatenate([w_in_[:, :, 0:2176], w_in_[:, :, 2208:4256]], axis=-1))
    kr = w_in_[:, :, 2176:2208]
    krs = np.concatenate([kr, kr[:, :, 16:32], kr[:, :, 0:16]], axis=-1)
    m["w_kr"] = np.ascontiguousarray(krs.reshape(L, 8, 128, 64).transpose(0, 2, 1, 3).reshape(L, 128, 512), dtype=f)
    wq4 = inp["w_q_up"].reshape(L, 384, 8, 96)
    wqs = np.concatenate([wq4, wq4[..., 80:96], wq4[..., 64:80]], axis=-1)
    m["w_q_b"] = np.ascontiguousarray(wqs.reshape(L, 3, 128, 8, 128).transpose(0, 3, 2, 1, 4).reshape(L, 8, 128, 384), dtype=f)
    wkv4 = inp["w_kv_up"].reshape(L, 2, 128, 8, 128)
    m["w_kn_b"] = np.ascontiguousarray(wkv4[..., 0:64].transpose(0, 3, 2, 1, 4).reshape(L, 8, 128, 128), dtype=f)
    wv = wkv4[..., 64:128].reshape(L, 2, 128, 4, 2, 64)
    m["w_kvv_b"] = np.ascontiguousarray(wv.transpose(0, 3, 2, 1, 4, 5).reshape(L, 4, 128, 256), dtype=f)
    so = inp["w_sb_out"].reshape(L, 4, 128, 8, 128); mo = inp["w_mla_out"].reshape(L, 4, 128, 8, 128)
    both = np.stack([so, mo], axis=4)
    m["w_o_b"] = np.ascontiguousarray(both.transpose(0, 3, 2, 1, 4, 5).reshape(L, 8, 128, 1024), dtype=f)
    m["w_mix_b"] = blocked(inp["w_mix_out"])
    m["w_up_b"] = blocked(inp["w_up"])
    m["w_dn_b"] = blocked(inp["w_down"])
    return m


def kernel(**inputs):
    inp = {k: np.asarray(v) for k, v in inputs.items()}
    if "nc" not in _CACHE:
        _CACHE["nc"] = build(list(range(L)), final_norm=True)[0]
    nc = _CACHE["nc"]
    sh = shared_inputs(inp)
    in_maps = []
    for b in range(8):
        m = dict(sh)
        m.update(host_inputs(inp, b))
        in_maps.append(m)
    res = run_bass_kernel_spmd(nc, in_maps, core_ids=list(range(8)))
    out = np.stack([np.ascontiguousarray(r["outT"].T) for r in res.results], axis=0)
    return out.astype(np.float32)
```

```python
import bisect
import numpy as np
import concourse.bass as bass
import concourse.mybir as mybir

F32 = mybir.dt.float32
BF16 = mybir.dt.bfloat16
I32 = mybir.dt.int32
AF = mybir.ActivationFunctionType
ALU = mybir.AluOpType
DTB = {F32: 4, BF16: 2, I32: 4}


class R:
    __slots__ = ("ap", "space", "p0", "p1", "b0", "b1")

    def __init__(self, ap, space, p0, p1, b0, b1):
        self.ap, self.space, self.p0, self.p1, self.b0, self.b1 = ap, space, p0, p1, b0, b1


class T:
    def __init__(self, arena_ap, space, off, shape, dtype, name=""):
        self.space, self.off, self.shape, self.dtype, self.name = space, off, list(shape), dtype, name
        self.esz = DTB[dtype]
        n = int(np.prod(shape))
        self.nbytes = n * self.esz
        assert off % 4 == 0 and self.nbytes % 4 == 0, (name, off, self.nbytes)
        base = arena_ap[:, off // 4:(off + self.nbytes) // 4]
        if dtype != F32:
            base = base.bitcast(dtype)
        if len(shape) == 2:
            base = base.rearrange("p (a b) -> p a b", a=shape[0], b=shape[1])
        elif len(shape) == 3:
            base = base.rearrange("p (a b c) -> p a b c", a=shape[0], b=shape[1], c=shape[2])
        elif len(shape) == 4:
            base = base.rearrange("p (a b c d) -> p a b c d", a=shape[0], b=shape[1], c=shape[2], d=shape[3])
        self.full = base
        st = []
        acc = 1
        for s in reversed(self.shape):
            st.append(acc)
            acc *= s
        self.strides = list(reversed(st))

    def __call__(self, *idx, p=(0, 128)):
        idx = list(idx) + [None] * (len(self.shape) - len(idx))
        key = [slice(p[0], p[1])]
        lo = 0
        hi = 0
        for i, (ix, s, st) in enumerate(zip(idx, self.shape, self.strides)):
            if ix is None:
                key.append(slice(None))
                hi += (s - 1) * st
            elif isinstance(ix, tuple):
                assert 0 <= ix[0] < ix[1] <= s, (self.name, idx)
                key.append(slice(ix[0], ix[1]))
                lo += ix[0] * st
                hi += (ix[1] - 1) * st
            else:
                assert 0 <= ix < s, (self.name, idx)
                key.append(ix)
                lo += ix * st
                hi += ix * st
        ap = self.full[tuple(key)]
        return R(ap, self.space, p[0], p[1], self.off + lo * self.esz, self.off + (hi + 1) * self.esz)


class _Track:
    def __init__(self, size):
        self.starts = [0]
        self.segs = [[0, size, -1, {}]]

    def _split(self, pos):
        i = bisect.bisect_right(self.starts, pos) - 1
        seg = self.segs[i]
        if seg[0] == pos or pos >= seg[1]:
            return
        new = [pos, seg[1], seg[2], dict(seg[3])]
        seg[1] = pos
        self.starts.insert(i + 1, pos)
        self.segs.insert(i + 1, new)

    def access(self, b0, b1, write, opidx, rkey, deps):
        self._split(b0)
        self._split(b1)
        i = bisect.bisect_left(self.starts, b0)
        n = len(self.segs)
        first = i
        while i < n and self.segs[i][0] < b1:
            seg = self.segs[i]
            if seg[2] >= 0:
                deps.add(seg[2])
            if write:
                deps.update(seg[3].values())
                seg[2] = opidx
                seg[3] = {}
            else:
                seg[3][rkey] = opidx
            i += 1
        if write and i - first > 1:
            keep = self.segs[first]
            keep[1] = self.segs[i - 1][1]
            del self.segs[first + 1:i]
            del self.starts[first + 1:i]


class Op:
    __slots__ = ("idx", "eng", "fn", "deps", "dma", "sig", "cnt", "sem", "out_dram")

    def __init__(self, idx, eng, fn, dma):
        self.idx, self.eng, self.fn, self.dma = idx, eng, fn, dma
        self.deps = set()
        self.sig = False
        self.cnt = 0
        self.sem = None
        self.out_dram = False


class Prog:
    ENGS = ("pe", "act", "dve", "pool", "sp")

    def __init__(self, nc, same_sync=True, n_dma_sems=12):
        self.nc = nc
        self.same_sync = same_sync
        self.n_dma_sems = n_dma_sems
        self.ops = []
        self.tracks = {}
        self.sizes = {}
        self.nrd = 0

    def add_space(self, name, size):
        self.sizes[name] = size
        self.tracks[(name, 0)] = _Track(size)
        self.tracks[(name, 1)] = _Track(size)

    def _touch(self, op, r, write):
        rkey = op.eng if not op.dma else ("dma", op.idx)
        for half in (0, 1):
            lo, hi = half * 64, half * 64 + 64
            if r.p0 < hi and r.p1 > lo:
                self.tracks[(r.space, half)].access(r.b0, r.b1, write, op.idx, rkey, op.deps)

    def op(self, eng, fn, r=(), w=()):
        o = Op(len(self.ops), eng, fn, False)
        for x in r:
            self._touch(o, x, False)
        for x in w:
            self._touch(o, x, True)
        o.deps.discard(o.idx)
        self.ops.append(o)
        return o

    def dma(self, queue, out, in_, r=(), w=(), out_dram=False):
        o = Op(len(self.ops), queue, lambda e, out=out, in_=in_: e.dma_start(out=out, in_=in_), True)
        o.out_dram = out_dram
        for x in r:
            self._touch(o, x, False)
        for x in w:
            self._touch(o, x, True)
        o.deps.discard(o.idx)
        self.ops.append(o)
        return o

    def finalize(self):
        nc = self.nc
        ops = self.ops
        engobj = {"pe": nc.tensor, "act": nc.scalar, "dve": nc.vector, "pool": nc.gpsimd, "sp": nc.sync}
        for o in ops:
            latest = {}
            for d in o.deps:
                od = ops[d]
                if od.dma:
                    continue
                if od.eng != o.eng or o.dma or (self.same_sync and od.eng != "pe"):
                    if latest.get(od.eng, -1) < d:
                        latest[od.eng] = d
            for d in latest.values():
                ops[d].sig = True
            o.deps = set(d for d in o.deps if ops[d].dma) | set(latest.values())
        self._ctx = []

        def mksem(name):
            g = nc.semaphore(name)
            s = g.__enter__()
            self._ctx.append(g)
            return s

        EP = 3968
        esem = {e: [] for e in self.ENGS}
        ecnt = {e: 0 for e in self.ENGS}
        dsem = {}
        dlast = {}
        dn = {}
        waited = {e: {} for e in self.ENGS}
        nwait = 0
        out_ops = []
        for o in ops:
            eng = engobj[o.eng]
            need = {}
            for d in o.deps:
                od = ops[d]
                if od.dma:
                    sem, val = od.sem, od.cnt
                else:
                    if not od.sig:
                        continue
                    if od.eng == o.eng and not o.dma and (od.eng == "pe" or not self.same_sync):
                        continue
                    sem, val = od.sem, od.cnt
                k = id(sem)
                if k not in need or need[k][1] < val:
                    need[k] = (sem, val)
            if o.dma:
                q = o.eng
                i = dn.get(q, 0)
                dn[q] = i + 1
                slot = (q, i % self.n_dma_sems)
                if slot not in dsem:
                    dsem[slot] = [mksem("d_%s_%d_%d" % (slot[0], slot[1], i)), 0]
                mysem, mycnt = dsem[slot]
                if mycnt > 0:
                    k = id(mysem)
                    if k not in need or need[k][1] < mycnt:
                        need[k] = (mysem, mycnt)
                if mycnt + 16 > EP:
                    dsem[slot] = [mksem("d_%s_%d_%d" % (slot[0], slot[1], i)), 0]
                    mysem, mycnt = dsem[slot]
            for k, (wsem, wval) in need.items():
                if waited[o.eng].get(k, 0) >= wval:
                    continue
                waited[o.eng][k] = wval
                eng.wait_ge(wsem, wval)
                nwait += 1
            ins = o.fn(eng)
            if o.dma:
                dsem[slot][1] += 16
                o.sem, o.cnt = mysem, dsem[slot][1]
                ins.then_inc(mysem, 16)
                if o.out_dram:
                    out_ops.append(o)
            elif o.sig:
                c0 = ecnt[o.eng]
                ecnt[o.eng] += 1
                if c0 % EP == 0:
                    esem[o.eng].append(mksem("s_%s_%d" % (o.eng, c0 // EP)))
                o.sem = esem[o.eng][-1]
                o.cnt = c0 % EP + 1
                ins.then_inc(o.sem, 1)
        for o in out_ops:
            k = id(o.sem)
            if waited["sp"].get(k, 0) >= o.cnt:
                continue
            waited["sp"][k] = o.cnt
            nc.sync.wait_ge(o.sem, o.cnt)
        self.stats = dict(n_ops=len(ops), n_wait=nwait, sig={e: ecnt[e] for e in self.ENGS},
                          per_eng={e: sum(1 for o in ops if o.eng == e) for e in self.ENGS})
        return self.stats


from concourse.bass_utils import run_bass_kernel_spmd

D = 1024; S = 2048; NG = 4; GW = 512; NKC = 8; DFF = 4096; L = 4
INW = 4256
SC_SB = 0.125
SC_MLA = 96 ** -0.5
EPS = 1e-6
NEGBIG = -30000.0
ARENA = 212480
STG = 2048


def build(layers, final_norm=True, same_sync=False, dbg=False, stop=None):
    nc = bass.Bass("TRN2", target_bir_lowering=False)

    def din(name, shape, dt=F32):
        return nc.dram_tensor(name, list(shape), dt, kind="ExternalInput").ap()

    xT_d = din("xT", [D, S]); cT_d = din("cT", [128, 8]); pos_d = din("pos", [128, S], I32)
    ropec_d = din("ropec", [128, 2]); consts_d = din("consts", [128, 7 * 128])
    vecs_d = din("vecs", [128, 4 * 69 + 8])
    w_ada = din("w_ada_b", [L, 48, 128, 1024]); w_in = din("w_in_b", [L, 33, 128, 1024]); w_kr = din("w_kr", [L, 128, 512])
    w_q = din("w_q_b", [L, 8, 128, 384]); w_kn = din("w_kn_b", [L, 8, 128, 128]); w_kvv = din("w_kvv_b", [L, 4, 128, 256])
    w_o = din("w_o_b", [L, 8, 128, 1024]); w_mix = din("w_mix_b", [L, 8, 128, 1024])
    w_up = din("w_up_b", [L, 32, 128, 1024]); w_dn = din("w_dn_b", [L, 8, 128, 4096])
    outT_d = nc.dram_tensor("outT", [D, S], F32, kind="ExternalOutput").ap()

    ga = nc.sbuf_tensor("arena", [128, ARENA // 4], F32); arena = ga.__enter__()
    gp = nc.psum_tensor("psum", [128, 4096], F32); psum = gp.__enter__()
    aa = arena[:, :]; pa = psum[:, :]
    P = Prog(nc, same_sync=same_sync, n_dma_sems=8)
    P.add_space("sb", ARENA); P.add_space("ps", 16384)
    PS = [T(pa, "ps", i * 2048, [512], F32, "ps%d" % i) for i in range(8)]
    psi = [0]

    def ps_next():
        b = PS[psi[0] % 6]; psi[0] += 1
        return b
    poi = [0]

    def po_next():
        b = PS[6 + poi[0] % 2]; poi[0] += 1
        return b

    cur = [0]
    dbg_outs = {}

    def tap(name, tile, dt):
        if not dbg:
            return
        n = int(np.prod(tile.shape))
        dt_ = nc.dram_tensor("dbg_" + name, [128, n], dt, kind="ExternalOutput").ap()
        flat = T(aa, "sb", tile.off, [n], dt, "flat_" + name)
        P.dma("sp", dt_[:, :], flat().ap, r=[flat()], out_dram=True)

    def alloc(shape, dt, name=""):
        t = T(aa, "sb", cur[0], shape, dt, name)
        cur[0] += (t.nbytes + 31) // 32 * 32
        assert cur[0] <= ARENA, (name, cur[0])
        return t

    def mm(out, lhsT, rhs, start, stop):
        P.op("pe", lambda e: e.matmul(out.ap, lhsT.ap, rhs.ap, start=start, stop=stop), r=[lhsT, rhs], w=[out])

    def act(out, in_, func, scale=1.0, bias=None, eng="act"):
        rs = [in_]
        kw = {}
        if isinstance(scale, R):
            rs.append(scale); kw["scale"] = scale.ap
        else:
            kw["scale"] = float(scale)
        if isinstance(bias, R):
            rs.append(bias); kw["bias"] = bias.ap
        elif bias is not None:
            kw["bias"] = float(bias)
        P.op("act", lambda e: e.activation(out=out.ap, in_=in_.ap, func=func, **kw), r=rs, w=[out])

    def tt(eng, out, a, b, op):
        P.op(eng, lambda e: e.tensor_tensor(out=out.ap, in0=a.ap, in1=b.ap, op=op), r=[a, b], w=[out])

    def stt(eng, out, in0, scalar, in1, op0, op1):
        rs = [in0, in1]
        sc = scalar
        if isinstance(scalar, R):
            rs.append(scalar); sc = scalar.ap
        P.op(eng, lambda e: e.scalar_tensor_tensor(out=out.ap, in0=in0.ap, scalar=sc, in1=in1.ap, op0=op0, op1=op1), r=rs, w=[out])

    def ts(eng, out, in0, s1, s2, op0, op1=None):
        rs = [in0]
        a1 = s1; a2 = s2
        if isinstance(s1, R):
            rs.append(s1); a1 = s1.ap
        if isinstance(s2, R):
            rs.append(s2); a2 = s2.ap
        if op1 is None:
            P.op(eng, lambda e: e.tensor_scalar(out=out.ap, in0=in0.ap, scalar1=a1, scalar2=None, op0=op0), r=rs, w=[out])
        else:
            P.op(eng, lambda e: e.tensor_scalar(out=out.ap, in0=in0.ap, scalar1=a1, scalar2=a2, op0=op0, op1=op1), r=rs, w=[out])

    def cp(eng, out, in_):
        P.op(eng, lambda e: e.tensor_copy(out=out.ap, in_=in_.ap), r=[in_], w=[out])

    def memset(eng, out, val):
        P.op(eng, lambda e: e.memset(out.ap, val), w=[out])

    xT = alloc([8, S], F32, "xT")
    hT = alloc([8, S], BF16, "hT")
    cos2 = alloc([S], BF16, "cos2"); sinS = alloc([S], BF16, "sinS")
    cb = alloc([7, 128], BF16, "cb")
    onesf = alloc([128], F32, "onesf")
    vecs = alloc([4 * 69 + 8], F32, "vecs")
    cact = alloc([8], BF16, "cact")
    mod = alloc([96], F32, "mod"); gs = alloc([32], F32, "gs")
    ropec = alloc([2], F32, "ropec")
    stg_off = [cur[0], cur[0] + STG * 4]; cur[0] += 2 * STG * 4
    stg_i = [0]
    epst = alloc([1], F32, "eps")
    DYN0 = cur[0]
    IDENT, NU, NONES, NEGS, NEGD, NEGF, ONESB = range(7)

    def wload(dst, src, shape):
        n = int(np.prod(shape)); assert n <= STG, shape
        st = T(aa, "sb", stg_off[stg_i[0] % 2], shape, F32, "stg"); stg_i[0] += 1
        P.dma("sp", st().ap, src, w=[st()])
        cp("dve", dst, st())

    def vec(l, off, n=1):
        return vecs((l * 69 + off, l * 69 + off + n))

    for c in range(8):
        for h2 in range(2):
            P.dma("sp", xT(c, (h2 * 1024, h2 * 1024 + 1024)).ap, xT_d[c * 128:(c + 1) * 128, h2 * 1024:(h2 + 1) * 1024],
                  w=[xT(c, (h2 * 1024, h2 * 1024 + 1024))])
    P.dma("sp", vecs().ap, vecs_d[:, :], w=[vecs()])
    P.dma("sp", ropec().ap, ropec_d[:, :], w=[ropec()])
    wload(cb(), consts_d[:, :].rearrange("p (a b) -> p a b", a=7, b=128), [7, 128])
    memset("dve", onesf(), 1.0)
    cur[0] = DYN0
    tc_ = alloc([8], F32, "tc"); tsg = alloc([8], F32, "tsg")
    P.dma("sp", tc_().ap, cT_d[:, :], w=[tc_()])
    act(tsg(), tc_(), AF.Sigmoid)
    tt("dve", cact(), tc_(), tsg(), ALU.mult)
    RP = (64, 96)
    posi = alloc([S], I32, "posi"); ang = alloc([S], F32, "ang"); u = alloc([S], F32, "u")
    ki = alloc([S], I32, "ki"); kf = alloc([S], F32, "kf"); r1 = alloc([S], F32, "r1"); r2 = alloc([S], F32, "r2")
    P.dma("sp", posi(p=RP).ap, pos_d[64:96, :], w=[posi(p=RP)])
    cp("dve", ang(p=RP), posi(p=RP))
    ts("dve", ang(p=RP), ang(p=RP), ropec((0, 1), p=RP), None, ALU.mult)
    TWO_PI = 2.0 * np.pi
    for which, dst in ((0, sinS), (1, cos2)):
        shift = 0.0 if which == 0 else np.pi / 2
        ts("dve", u(p=RP), ang(p=RP), float(shift), float(1.0 / TWO_PI), ALU.add, ALU.mult)
        cp("dve", ki(p=RP), u(p=RP))
        cp("dve", kf(p=RP), ki(p=RP))
        stt("dve", r1(p=RP), kf(p=RP), float(-TWO_PI), ang(p=RP), ALU.mult, ALU.add)
        if which == 1:
            ts("dve", r1(p=RP), r1(p=RP), float(np.pi / 2), None, ALU.add)
        ts("dve", r2(p=RP), r1(p=RP), float(np.pi), float(-TWO_PI), ALU.is_gt, ALU.mult)
        tt("dve", r1(p=RP), r1(p=RP), r2(p=RP), ALU.add)
        ts("dve", r2(p=RP), r1(p=RP), float(-np.pi), float(TWO_PI), ALU.is_lt, ALU.mult)
        tt("dve", r1(p=RP), r1(p=RP), r2(p=RP), ALU.add)
        ts("dve", r1(p=RP), r1(p=RP), float(3.141592), float(-3.141592), ALU.min, ALU.max)
        if which == 0:
            act(r2(p=RP), r1(p=RP), AF.Sin)
            ts("dve", sinS(p=RP), r2(p=RP), ropec((1, 2), p=RP), None, ALU.mult)
        else:
            act(cos2(p=RP), r1(p=RP), AF.Sin)

    def rstd_from_ss(ps_ss, dst, n):
        act(dst, ps_ss, AF.Ln, scale=1.0 / n, bias=epsb())
        act(dst, dst, AF.Exp, scale=-0.5)

    def epsb():
        return epsc()

    def norm_mod(l, which, sq, rstd, tmp):
        for g in range(NG):
            tg = (g * GW, (g + 1) * GW)
            pss = ps_next()
            for c in range(8):
                act(sq[c % 2](), xT(c, tg), AF.Square)
                mm(pss(), cb(ONESB), sq[c % 2](), c == 0, c == 7)
            rstd_from_ss(pss(), rstd(), D)
            for c in range(8):
                gi = (l % 2) * 16 + which * 8 + c
                mi = (l % 2) * 48 + which * 24 + c
                stt("dve", tmp[c % 2](), xT(c, tg), gs((gi, gi + 1)), rstd(), ALU.mult, ALU.mult)
                act(hT(c, tg), tmp[c % 2](), AF.Identity, bias=mod((mi, mi + 1)))

    memset("dve", epst(), EPS)

    def epsc():
        return epst()
    DYN1 = DYN0

    def mod_work(l):
        wa = [alloc([8, 128], BF16, "wa%d" % i) for i in range(2)]
        psm = po_next()
        mb = (l % 2) * 48
        gb = (l % 2) * 16
        for col in range(48):
            w_ = wa[col % 2]
            wload(w_(), w_ada[l, col].rearrange("p (a b) -> p a b", a=8, b=128), [8, 128])
            for kc in range(8):
                mm(psm((col, col + 1)), w_(kc), cact((kc, kc + 1)), kc == 0, kc == 7)
            yield col
        tt("dve", mod((mb, mb + 48)), psm((0, 48)), vec(l, 21, 48), ALU.add)
        stt("dve", gs((gb, gb + 8)), mod((mb + 8, mb + 16)), 1.0, vec(l, 0, 8), ALU.add, ALU.mult)
        stt("dve", gs((gb + 8, gb + 16)), mod((mb + 32, mb + 40)), 1.0, vec(l, 8, 8), ALU.add, ALU.mult)

    def layer(l):
        mb = (l % 2) * 48
        gb = (l % 2) * 16
        cur[0] = DYN1
        if l == layers[0]:
            for _ in mod_work(l):
                pass
        if stop == 'mod':
            return
        cur[0] = DYN1
        sq = [alloc([GW], BF16, "sq%d" % i) for i in range(2)]
        rstd = alloc([GW], F32, "rstd")
        tmp = [alloc([GW], F32, "tmp%d" % i) for i in range(2)]
        norm_mod(l, 0, sq, rstd, tmp)

        tap("mod", mod, F32); tap("hT", hT, BF16)
        if stop == 'norm1':
            return
        cur[0] = DYN1
        aoM = alloc([4, S], BF16, "aoM")
        MLA0 = cur[0]
        nq = alloc([3, S], BF16, "nq"); nkv = alloc([2, S], BF16, "nkv")
        kpe = alloc([S], BF16, "kpe")
        t1 = alloc([GW], F32, "t1"); t2 = alloc([GW], F32, "t2")
        LAT0 = cur[0]
        wl = alloc([8, 128], BF16, "wl")
        sqb = [alloc([GW], BF16, "sqb%d" % i) for i in range(2)]
        rs2 = alloc([GW], F32, "rs2")
        latps = {}
        for j in range(5):
            wload(wl(), w_in[l, 12 + j].rearrange("p (a b) -> p a b", a=8, b=128), [8, 128])
            for g in range(NG):
                tg = (g * GW, (g + 1) * GW)
                pb = ps_next()
                for kc in range(8):
                    mm(pb(), wl(kc), hT(kc, tg), kc == 0, kc == 7)
                dst = nq(j, tg) if j < 3 else nkv(j - 3, tg)
                act(dst, pb(), AF.Copy)
        for (tile, nch, goff) in ((nq, 3, 16), (nkv, 2, 19)):
            for g in range(NG):
                tg = (g * GW, (g + 1) * GW)
                pss = ps_next()
                for j in range(nch):
                    act(sqb[j % 2](), tile(j, tg), AF.Square)
                    mm(pss(), cb(ONESB), sqb[j % 2](), j == 0, j == nch - 1)
                rstd_from_ss(pss(), rs2(), nch * 128)
                for j in range(nch):
                    stt("dve", tile(j, tg), tile(j, tg), vec(l, goff + j, 1), rs2(), ALU.mult, ALU.mult)
        wkr = alloc([8, 64], BF16, "wkr")
        wload(wkr(), w_kr[l].rearrange("p (a b) -> p a b", a=8, b=64), [8, 64])
        for g in range(NG):
            tg = (g * GW, (g + 1) * GW)
            pA = ps_next(); pB = ps_next()
            for kc in range(8):
                mm(pA(p=RP), wkr(kc, (0, 32)), hT(kc, tg), kc == 0, kc == 7)
            for kc in range(8):
                mm(pB(p=RP), wkr(kc, (32, 64)), hT(kc, tg), kc == 0, kc == 7)
            tt("dve", t1(p=RP), pA(p=RP), cos2(tg, p=RP), ALU.mult)
            tt("dve", t2(p=RP), pB(p=RP), sinS(tg, p=RP), ALU.mult)
            tt("dve", kpe(tg, p=RP), t1(p=RP), t2(p=RP), ALU.add)
        tap("nq", nq, BF16); tap("nkv", nkv, BF16); tap("kpe", kpe, BF16); tap("cos2", cos2, BF16); tap("sinS", sinS, BF16)
        if stop == 'lat':
            return
        cur[0] = LAT0
        wq = [alloc([3, 128], BF16, "wq%d" % i) for i in range(2)]
        wkn = [alloc([2, 64], BF16, "wkn%d" % i) for i in range(2)]
        wkvv = alloc([2, 128], BF16, "wkvv")
        qh = [alloc([S], BF16, "qh%d" % i) for i in range(2)]; kh_ = [alloc([S], BF16, "kh%d" % i) for i in range(2)]
        vp = [alloc([16, 2, 66], BF16, "vp%d" % i) for i in range(2)]
        AT = [alloc([GW], BF16, "AT%d" % i) for i in range(3)]
        rec = alloc([GW], F32, "rec"); rb = alloc([GW], F32, "rb")
        aot = alloc([S], BF16, "aot")

        def mla_proj(h):
            sl = h % 2
            wq_, wkn_, qh_, kk_ = wq[sl], wkn[sl], qh[sl], kh_[sl]
            wload(wq_(), w_q[l, h].rearrange("p (a b) -> p a b", a=3, b=128), [3, 128])
            wload(wkn_(), w_kn[l, h].rearrange("p (a b) -> p a b", a=2, b=64), [2, 64])
            if h % 2 == 0:
                vp_ = vp[(h // 2) % 2]
                wload(wkvv(), w_kvv[l, h // 2].rearrange("p (a b) -> p a b", a=2, b=128), [2, 128])
                memset("pool", vp_(None, None, (64, 65)), 1.0)
                for tt_ in range(16):
                    pv = ps_next()
                    for kc in range(2):
                        mm(pv((0, 128)), nkv(kc, (tt_ * 128, tt_ * 128 + 128)), wkvv(kc), kc == 0, kc == 1)
                    P.op("dve", lambda e, pv=pv, tt_=tt_, vp_=vp_: e.tensor_copy(
                        out=vp_(tt_, None, (0, 64)).ap, in_=pv((0, 128)).ap.rearrange("p (a b) -> p a b", a=2, b=64)),
                        r=[pv((0, 128))], w=[vp_(tt_)])
            for g in range(NG):
                tg = (g * GW, (g + 1) * GW)
                pq = ps_next(); pq2 = ps_next(); pk = ps_next()
                for kc in range(3):
                    mm(pq(p=(0, 96)), wq_(kc, (0, 96)), nq(kc, tg), kc == 0, kc == 2)
                for kc in range(3):
                    mm(pq2(p=RP), wq_(kc, (96, 128)), nq(kc, tg), kc == 0, kc == 2)
                for kc in range(2):
                    mm(pk(p=(0, 64)), wkn_(kc), nkv(kc, tg), kc == 0, kc == 1)
                cp("dve", qh_(tg, p=(0, 64)), pq(p=(0, 64)))
                tt("dve", t1(p=RP), pq(p=RP), cos2(tg, p=RP), ALU.mult)
                tt("dve", t2(p=RP), pq2(p=RP), sinS(tg, p=RP), ALU.mult)
                tt("dve", qh_(tg, p=RP), t1(p=RP), t2(p=RP), ALU.add)
                cp("dve", kk_(tg, p=(0, 64)), pk(p=(0, 64)))
                cp("dve", kk_(tg, p=RP), kpe(tg, p=RP))

        def mla_attn(h):
            sl = h % 2
            hp, hb = h // 2, (h % 2) * 64
            qh_, kk_, vp_ = qh[sl], kh_[sl], vp[hp % 2]
            items = []
            for g in range(NG):
                nkb = 4 * g + 4
                for i, kb in enumerate(range(nkb - 1, -1, -1)):
                    items.append(dict(g=g, i=i, kb=kb, nkb=nkb))
            N = len(items)
            pos_ = {}

            def s1(n):
                it = items[n]
                g, kb = it["g"], it["kb"]
                tg = (g * GW, (g + 1) * GW)
                pz = ps_next()
                dg = kb - 4 * g
                mm(pz(), kk_((kb * 128, kb * 128 + 128), p=(0, 96)), qh_(tg, p=(0, 96)), True, dg < 0)
                if dg >= 0:
                    for qb in range(dg + 1):
                        mm(pz((qb * 128, qb * 128 + 128)), cb(IDENT), cb(NEGD if qb == dg else NEGF), False, qb == dg)
                act(AT[n % 3](), pz(), AF.Exp, scale=SC_MLA)

            def s2(n):
                it = items[n]
                g, kb, i, nkb = it["g"], it["kb"], it["i"], it["nkb"]
                tg = (g * GW, (g + 1) * GW)
                if i == 0:
                    pos_[g] = po_next()
                po = pos_[g]
                mm(po(p=(0, 65)), vp_(kb, h % 2, (0, 65)), AT[n % 3](), i == 0, i == nkb - 1)
                if i == nkb - 1:
                    act(rec(p=(64, 65)), po(p=(64, 65)), AF.Ln)
                    act(rec(p=(64, 65)), rec(p=(64, 65)), AF.Exp, scale=-1.0)
                    pend.append((n, g, po))

            def s3(g, po):
                tg = (g * GW, (g + 1) * GW)
                pr = ps_next()
                mm(pr(p=(0, 64)), onesf((0, 64), p=(64, 65)), rec(p=(64, 65)), True, True)
                cp("dve", rb(p=(0, 64)), pr(p=(0, 64)))
                if hb == 0:
                    tt("dve", aoM(hp, tg, p=(0, 64)), po(p=(0, 64)), rb(p=(0, 64)), ALU.mult)
                else:
                    tt("dve", aot(tg, p=(0, 64)), po(p=(0, 64)), rb(p=(0, 64)), ALU.mult)
            pend = []
            for step in range(N + 1):
                if step < N:
                    s1(step)
                if step >= 1:
                    s2(step - 1)
                if pend and step - 1 - pend[0][0] >= 3:
                    _, g_, po_ = pend.pop(0)
                    s3(g_, po_)
            while pend:
                _, g_, po_ = pend.pop(0)
                s3(g_, po_)
            if hb == 64:
                P.dma("pool", aoM(hp, p=(64, 128)).ap, aot(p=(0, 64)).ap, r=[aot(p=(0, 64))], w=[aoM(hp, p=(64, 128))])

        mla_proj(0)
        for h in range(8):
            if h + 1 < 8:
                mla_proj(h + 1)
            mla_attn(h)

        tap("aoM", aoM, BF16); tap("qh", qh[1], BF16); tap("kh", kh_[1], BF16)
        if stop == 'mla':
            return
        cur[0] = MLA0
        aoS = alloc([4, S], BF16, "aoS")
        MLA0 = cur[0]
        wqkv = [alloc([8, 3, 128], BF16, "wqkv%d" % i) for i in range(2)]
        qp = [alloc([S], BF16, "qp%d" % i) for i in range(2)]; kp = [alloc([S], BF16, "kp%d" % i) for i in range(2)]
        vs = [alloc([16, 128], BF16, "vs%d" % i) for i in range(2)]
        ef = [alloc([GW], F32, "ef%d" % i) for i in range(2)]
        sp_ = [alloc([GW], BF16, "sp%d" % i) for i in range(4)]
        acc = [alloc([GW], BF16, "acc%d" % i) for i in range(2)]
        AT = [alloc([GW], BF16, "ATs%d" % i) for i in range(3)]

        def sb_proj(hp):
            sl = hp % 2
            w_, qp_, kp_, vs_ = wqkv[sl], qp[sl], kp[sl], vs[sl]
            for j in range(3):
                wload(w_(None, j), w_in[l, j * 4 + hp].rearrange("p (a b) -> p a b", a=8, b=128), [8, 128])
            for g in range(NG):
                tg = (g * GW, (g + 1) * GW)
                pq = ps_next(); pk = ps_next()
                for kc in range(8):
                    mm(pq(), w_(kc, 0), hT(kc, tg), kc == 0, kc == 7)
                for kc in range(8):
                    mm(pk(), w_(kc, 1), hT(kc, tg), kc == 0, kc == 7)
                cp("dve", qp_(tg), pq())
                cp("dve", kp_(tg), pk())
            for tt_ in range(16):
                pv = ps_next()
                for kc in range(8):
                    mm(pv((0, 128)), hT(kc, (tt_ * 128, tt_ * 128 + 128)), w_(kc, 2), kc == 0, kc == 7)
                cp("dve", vs_(tt_), pv((0, 128)))

        def sb_attn(hp):
            sl = hp % 2
            qp_, kp_, vs_ = qp[sl], kp[sl], vs[sl]
            items = []
            for hh in range(2):
                for g in range(NG):
                    nkb = 4 * g + 4
                    for i, kb in enumerate(range(nkb - 1, -1, -1)):
                        items.append(dict(hh=hh, g=g, i=i, kb=kb, nkb=nkb))
            N = len(items)
            pos_ = {}

            def zmm(pz, it, last_stop):
                hh, g, kb = it["hh"], it["g"], it["kb"]
                hr = (hh * 64, hh * 64 + 64)
                tg = (g * GW, (g + 1) * GW)
                dg = kb - 4 * g
                mm(pz(), kp_((kb * 128, kb * 128 + 128), p=hr), qp_(tg, p=hr), True, last_stop and dg < 0)
                if dg >= 0:
                    for qb in range(dg + 1):
                        mm(pz((qb * 128, qb * 128 + 128)), cb(IDENT), cb(NEGS if qb == dg else NEGF), False, last_stop and qb == dg)

            def s1(n):
                it = items[n]
                pz = ps_next()
                zmm(pz, it, True)
                e_ = ef[n % 2]
                act(e_(), pz(), AF.Exp, scale=SC_SB)
                act(sp_[n % 4](), e_(), AF.Ln, bias=1.0)

            def s2(n):
                it = items[n]
                i = it["i"]
                s_ = sp_[n % 4]
                pz2 = ps_next()
                zmm(pz2, it, False)
                mm(pz2(), cb(NU), s_(), False, i == 0)
                if i == 1:
                    mm(pz2(), cb(NONES), sp_[(n - 1) % 4](), False, True)
                    tt("dve", acc[i % 2](), sp_[(n - 1) % 4](), s_(), ALU.add)
                elif i > 1:
                    mm(pz2(), cb(NONES), acc[(i - 1) % 2](), False, True)
                    tt("dve", acc[i % 2](), acc[(i - 1) % 2](), s_(), ALU.add)
                act(AT[n % 3](), pz2(), AF.Exp, scale=SC_SB)

            def s3(n):
                it = items[n]
                hh, g, i, kb, nkb = it["hh"], it["g"], it["i"], it["kb"], it["nkb"]
                hr = (hh * 64, hh * 64 + 64)
                tg = (g * GW, (g + 1) * GW)
                if i == 0:
                    pos_[(hh, g)] = po_next()
                po = pos_[(hh, g)]
                mm(po(p=hr), vs_(kb, (hh * 64, hh * 64 + 64)), AT[n % 3](), i == 0, i == nkb - 1)
                if i == nkb - 1:
                    cp("dve", aoS(hp, tg, p=hr), po(p=hr))
            for step in range(N + 2):
                if step < N:
                    s1(step)
                if 1 <= step <= N:
                    s2(step - 1)
                if step >= 2:
                    s3(step - 2)

        sb_proj(0)
        for hp in range(4):
            if hp + 1 < 4:
                sb_proj(hp + 1)
            sb_attn(hp)

        tap("aoS", aoS, BF16)
        if stop == 'sb':
            return
        cur[0] = MLA0
        mg = alloc([8, S], BF16, "mg")
        MG_END = cur[0]
        wgs = [alloc([8, 2, 128], BF16, "wg%d" % i) for i in range(2)]; wos = [alloc([4, 2, 128], BF16, "wo0")] * 2
        s1 = alloc([GW], F32, "s1"); s2 = alloc([GW], F32, "s2"); m1 = s1; m2 = s2
        nxt = mod_work(l + 1) if (l + 1 < L and l + 1 in layers) else None
        def mg_load_g(c_):
            for j in range(2):
                wload(wgs[c_ % 2](None, j), w_in[l, 17 + j * 8 + c_].rearrange("p (a b) -> p a b", a=8, b=128), [8, 128])

        def mg_load_o(c_):
            wload(wos[0](), w_o[l, c_].rearrange("p (a b c) -> p a b c", a=4, b=2, c=128), [4, 2, 128])
        mg_load_g(0); mg_load_o(0)
        for c in range(8):
            wg = wgs[c % 2]; wo = wos[c % 2]
            if c + 1 < 8:
                mg_load_g(c + 1)
            for g in range(NG):
                tg = (g * GW, (g + 1) * GW)
                pg1 = ps_next(); pg2 = ps_next(); po1 = ps_next(); po2 = ps_next()
                for kc in range(8):
                    mm(pg1(), wg(kc, 0), hT(kc, tg), kc == 0, kc == 7)
                for kc in range(8):
                    mm(pg2(), wg(kc, 1), hT(kc, tg), kc == 0, kc == 7)
                for kc in range(4):
                    mm(po1(), wo(kc, 0), aoS(kc, tg), kc == 0, kc == 3)
                for kc in range(4):
                    mm(po2(), wo(kc, 1), aoM(kc, tg), kc == 0, kc == 3)
                if g == NG - 1 and c + 1 < 8:
                    mg_load_o(c + 1)
                act(s1(), pg1(), AF.Sigmoid)
                act(s2(), pg2(), AF.Sigmoid)
                tt("dve", m1(), s1(), po1(), ALU.mult)
                tt("dve", m2(), s2(), po2(), ALU.mult)
                tt("dve", mg(c, tg), m1(), m2(), ALU.add)
                if nxt is not None:
                    for _ in range(2):
                        next(nxt, None)

        tap("mg", mg, BF16)
        if stop == 'merge':
            return
        cur[0] = MG_END
        wm = [alloc([8, 128], BF16, "wm%d" % i) for i in range(2)]
        wload(wm[0](), w_mix[l, 0].rearrange("p (a b) -> p a b", a=8, b=128), [8, 128])
        for c in range(8):
            w_ = wm[c % 2]
            if c + 1 < 8:
                wload(wm[(c + 1) % 2](), w_mix[l, c + 1].rearrange("p (a b) -> p a b", a=8, b=128), [8, 128])
            for g in range(NG):
                tg = (g * GW, (g + 1) * GW)
                pb = ps_next()
                for kc in range(8):
                    mm(pb(), w_(kc), mg(kc, tg), kc == 0, kc == 7)
                stt("dve", xT(c, tg), pb(), mod((mb + 16 + c, mb + 17 + c)), xT(c, tg), ALU.mult, ALU.add)

        if stop == 'mix':
            return
        cur[0] = DYN1
        sq = [alloc([GW], BF16, "sq%d" % i) for i in range(2)]
        rstd = alloc([GW], F32, "rstd")
        tmp = [alloc([GW], F32, "tmp%d" % i) for i in range(2)]
        norm_mod(l, 1, sq, rstd, tmp)

        if stop == 'norm2':
            return
        cur[0] = DYN1
        upT = alloc([32, 1024], BF16, "upT")
        wu = [alloc([8, 128], BF16, "wu%d" % i) for i in range(2)]
        wd = alloc([32, 128], BF16, "wd")
        rl = [alloc([GW], F32, "rl%d" % i) for i in range(2)]
        for th in range(2):
            wload(wu[0](), w_up[l, 0].rearrange("p (a b) -> p a b", a=8, b=128), [8, 128])
            for fc in range(32):
                w_ = wu[fc % 2]
                if fc + 1 < 32:
                    wload(wu[(fc + 1) % 2](), w_up[l, fc + 1].rearrange("p (a b) -> p a b", a=8, b=128), [8, 128])
                for g2 in range(2):
                    tg = (th * 1024 + g2 * GW, th * 1024 + (g2 + 1) * GW)
                    pb = ps_next()
                    for kc in range(8):
                        mm(pb(), w_(kc), hT(kc, tg), kc == 0, kc == 7)
                    r_ = rl[(fc * 2 + g2) % 2]
                    act(r_(), pb(), AF.Relu)
                    tt("dve", upT(fc, (g2 * GW, (g2 + 1) * GW)), r_(), r_(), ALU.mult)
            def wd_load(c_, q2):
                wdb = w_dn[l, c_].rearrange("p (a b) -> p a b", a=32, b=128)
                wload(wd((q2 * 16, q2 * 16 + 16)), wdb[:, q2 * 16:(q2 + 1) * 16, :], [16, 128])
            wd_load(0, 0); wd_load(0, 1)
            for c in range(8):
                pbs = [ps_next(), ps_next()]
                for q2 in range(2):
                    for g2 in range(2):
                        for kc in range(q2 * 16, q2 * 16 + 16):
                            mm(pbs[g2](), wd(kc), upT(kc, (g2 * GW, (g2 + 1) * GW)), kc == 0, kc == 31)
                    if c + 1 < 8:
                        wd_load(c + 1, q2)
                for g2 in range(2):
                    tg = (th * 1024 + g2 * GW, th * 1024 + (g2 + 1) * GW)
                    stt("dve", xT(c, tg), pbs[g2](), mod((mb + 40 + c, mb + 41 + c)), xT(c, tg), ALU.mult, ALU.add)

    for l in layers:
        if stop != 'pro':
            layer(l)

    cur[0] = DYN1
    sq = [alloc([GW], BF16, "sq%d" % i) for i in range(2)]
    rstd = alloc([GW], F32, "rstd")
    ot = [alloc([GW], F32, "ot%d" % i) for i in range(2)]
    for g in range(NG):
        tg = (g * GW, (g + 1) * GW)
        if final_norm:
            pss = ps_next()
            for c in range(8):
                act(sq[c % 2](), xT(c, tg), AF.Square)
                mm(pss(), cb(ONESB), sq[c % 2](), c == 0, c == 7)
            rstd_from_ss(pss(), rstd(), D)
        for c in range(8):
            o_ = ot[c % 2]
            if final_norm:
                stt("dve", o_(), xT(c, tg), vecs((4 * 69 + c, 4 * 69 + c + 1)), rstd(), ALU.mult, ALU.mult)
                P.dma("sp", outT_d[c * 128:(c + 1) * 128, g * GW:(g + 1) * GW], o_().ap, r=[o_()], out_dram=True)
            else:
                P.dma("sp", outT_d[c * 128:(c + 1) * 128, g * GW:(g + 1) * GW], xT(c, tg).ap, r=[xT(c, tg)], out_dram=True)
    stats = P.finalize()
    return nc, stats


_CACHE = {}


def host_inputs(inp, b):
    f = np.float32
    m = {}
    m["xT"] = np.ascontiguousarray(inp["x"][b].T)
    m["cT"] = np.ascontiguousarray(inp["c"][b].reshape(8, 128).T)
    m["pos"] = np.ascontiguousarray(np.broadcast_to(inp["positions"][b][None, :], (128, S))).astype(np.int32)
    return m


def shared_inputs(inp):
    f = np.float32
    m = {}
    ropec = np.zeros((128, 2), f)
    inv = (1.0 / (np.float32(10000.0) ** (np.arange(0, 32, 2, dtype=f) / np.float32(32)))).astype(f)
    for p in range(64, 96):
        ropec[p, 0] = inv[(p - 64) % 16]
        ropec[p, 1] = -1.0 if p < 80 else 1.0
    m["ropec"] = ropec
    j = np.arange(128)[:, None]; t = np.arange(128)[None, :]
    consts = np.zeros((128, 7, 128), f)
    consts[:, 0] = (j == t)
    consts[:, 1] = np.where(j >= t, -1.0 / SC_SB, 0.0)
    consts[:, 2] = -1.0 / SC_SB
    consts[:, 3] = np.where(j >= t, NEGBIG, 0.0)
    consts[:, 4] = np.where(j > t, NEGBIG, 0.0)
    consts[:, 5] = NEGBIG
    consts[:, 6] = 1.0
    m["consts"] = consts.reshape(128, 7 * 128)
    vecs = np.zeros((128, 4 * 69 + 8), f)
    for l in range(L):
        o = l * 69
        vecs[:, o:o + 8] = inp["g_mix_norm"][l].reshape(8, 128).T
        vecs[:, o + 8:o + 16] = inp["g_mlp_norm"][l].reshape(8, 128).T
        vecs[:, o + 16:o + 19] = inp["g_q_lat"][l].reshape(3, 128).T
        vecs[:, o + 19:o + 21] = inp["g_kv_lat"][l].reshape(2, 128).T
        vecs[:, o + 21:o + 69] = inp["b_ada"][l].reshape(48, 128).T
    vecs[:, 4 * 69:] = inp["g_final"].reshape(8, 128).T
    m["vecs"] = vecs
    def blocked(w, nb_cols=None):
        Lh, K, N = w.shape
        return np.ascontiguousarray(w.reshape(Lh, K // 128, 128, N // 128, 128).transpose(0, 3, 2, 1, 4).reshape(Lh, N // 128, 128, (K // 128) * 128), dtype=f)

    w_in_ = inp["w_in"]
    m["w_ada_b"] = blocked(inp["w_ada"])
    m["w_in_b"] = blocked(np.concatenate([w_in_[:, :, 0:2176], w_in_[:, :, 2208:4256]], axis=-1))
    kr = w_in_[:, :, 2176:2208]
    krs = np.concatenate([kr, kr[:, :, 16:32], kr[:, :, 0:16]], axis=-1)
    m["w_kr"] = np.ascontiguousarray(krs.reshape(L, 8, 128, 64).transpose(0, 2, 1, 3).reshape(L, 128, 512), dtype=f)
    wq4 = inp["w_q_up"].reshape(L, 384, 8, 96)
    wqs = np.concatenate([wq4, wq4[..., 80:96], wq4[..., 64:80]], axis=-1)
    m["w_q_b"] = np.ascontiguousarray(wqs.reshape(L, 3, 128, 8, 128).transpose(0, 3, 2, 1, 4).reshape(L, 8, 128, 384), dtype=f)
    wkv4 = inp["w_kv_up"].reshape(L, 2, 128, 8, 128)
    m["w_kn_b"] = np.ascontiguousarray(wkv4[..., 0:64].transpose(0, 3, 2, 1, 4).reshape(L, 8, 128, 128), dtype=f)
    wv = wkv4[..., 64:128].reshape(L, 2, 128, 4, 2, 64)
    m["w_kvv_b"] = np.ascontiguousarray(wv.transpose(0, 3, 2, 1, 4, 5).reshape(L, 4, 128, 256), dtype=f)
    so = inp["w_sb_out"].reshape(L, 4, 128, 8, 128); mo = inp["w_mla_out"].reshape(L, 4, 128, 8, 128)
    both = np.stack([so, mo], axis=4)
    m["w_o_b"] = np.ascontiguousarray(both.transpose(0, 3, 2, 1, 4, 5).reshape(L, 8, 128, 1024), dtype=f)
    m["w_mix_b"] = blocked(inp["w_mix_out"])
    m["w_up_b"] = blocked(inp["w_up"])
    m["w_dn_b"] = blocked(inp["w_down"])
    return m


def kernel(**inputs):
    inp = {k: np.asarray(v) for k, v in inputs.items()}
    if "nc" not in _CACHE:
        _CACHE["nc"] = build(list(range(L)), final_norm=True)[0]
    nc = _CACHE["nc"]
    sh = shared_inputs(inp)
    in_maps = []
    for b in range(8):
        m = dict(sh)
        m.update(host_inputs(inp, b))
        in_maps.append(m)
    res = run_bass_kernel_spmd(nc, in_maps, core_ids=list(range(8)))
    out = np.stack([np.ascontiguousarray(r["outT"].T) for r in res.results], axis=0)
    return out.astype(np.float32)
```
